# Optimizing a Trainium2 kernel written in Bass

```python
import math
import jax
import jax.numpy as jnp
from jax import lax
import numpy as np


D_MODEL = 1024
BATCH = 8
SEQ = 4096
DEPTH = 4

DIFF_HEADS = 4
DIFF_QK_DIM = 64
DIFF_V_DIM = 2 * DIFF_QK_DIM
DIFF_WIDTH = DIFF_HEADS * DIFF_V_DIM
RET_HEADS = 8
RET_QK_DIM = 64
RET_V_DIM = 64
RET_WIDTH = RET_HEADS * RET_V_DIM
MIX_WIDTH = DIFF_WIDTH + RET_WIDTH
D_FF = 4 * D_MODEL
Q_BLOCK = 128
RET_CHUNK = 128
NORM_EPS = 1e-6
IN_SIZES = (DIFF_HEADS * 2 * DIFF_QK_DIM, DIFF_HEADS * 2 * DIFF_QK_DIM, DIFF_WIDTH,
            RET_HEADS * RET_QK_DIM, RET_HEADS * RET_QK_DIM, RET_WIDTH, RET_WIDTH)
IN_WIDTH = sum(IN_SIZES)

kernel_name = 'hybrid_diffattn_retention_encoder'


def rms_norm(x, g):
    xf = x.astype(jnp.float32)
    y = xf * lax.rsqrt(jnp.mean(xf * xf, axis=-1, keepdims=True) + NORM_EPS)
    return (y * g.astype(jnp.float32)).astype(x.dtype)


def group_norm(x, g):
    xf = x.astype(jnp.float32)
    mu = jnp.mean(xf, axis=-1, keepdims=True)
    xc = xf - mu
    var = jnp.mean(xc * xc, axis=-1, keepdims=True)
    return (xc * lax.rsqrt(var + NORM_EPS) * g.astype(jnp.float32)).astype(x.dtype)


def lambda_init(layer):
    return 0.8 - 0.6 * math.exp(-0.3 * layer)


def diff_attention(q, k, v, lam, slopes):
    B, H, _, S, dk = q.shape
    dv = v.shape[-1]
    nb = S // Q_BLOCK
    qb = q.reshape(B, H, 2, nb, Q_BLOCK, dk).transpose(3, 0, 1, 2, 4, 5)
    starts = jnp.arange(nb, dtype=jnp.int32) * Q_BLOCK
    kpos = jnp.arange(S, dtype=jnp.int32)
    scale = dk ** -0.5

    def block(args):
        q_blk, start = args
        qpos = start + jnp.arange(Q_BLOCK, dtype=jnp.int32)
        dist = jnp.abs(qpos[:, None] - kpos[None, :]).astype(jnp.float32)
        bias = -slopes[:, None, None] * dist
        s = jnp.einsum('bhcqd,bhckd->bhcqk', q_blk, k).astype(jnp.float32) * scale
        p = jax.nn.softmax(s + bias[None, :, None], axis=-1)
        a = p[:, :, 0] - lam[None, :, None, None] * p[:, :, 1]
        return jnp.einsum('bhqk,bhkd->bhqd', a.astype(v.dtype), v)

    out = lax.map(block, (qb, starts))
    return out.transpose(1, 0, 3, 2, 4).reshape(B, S, H, dv)


def retention_scan(q, k, v, log_g, strict):
    B, H, S, dk = q.shape
    dv = v.shape[-1]
    C = RET_CHUNK
    n = S // C
    qc = q.reshape(B, H, n, C, dk)
    kc = k.reshape(B, H, n, C, dk)
    vc = v.reshape(B, H, n, C, dv)
    idx = jnp.arange(C, dtype=jnp.float32)
    rel = idx[:, None] - idx[None, :]
    mask = (rel > 0) if strict else (rel >= 0)
    decay = jnp.where(mask, jnp.exp(log_g[:, None, None] * jnp.maximum(rel, 0.0)), 0.0).astype(v.dtype)
    scores = jnp.einsum('bhncd,bhnjd->bhncj', qc, kc) * decay[None, :, None]
    intra = jnp.einsum('bhncj,bhnje->bhnce', scores, vc)
    k_w = jnp.exp(log_g[:, None] * (C - 1.0 - idx)[None, :]).astype(v.dtype)
    chunk_kv = jnp.einsum('bhnjd,hj,bhnje->nbhde', kc, k_w, vc)
    g_chunk = jnp.exp(log_g * C).astype(v.dtype)[None, :, None, None]

    def step(state, kv):
        return g_chunk * state + kv, state

    _, prev = lax.scan(step, jnp.zeros((B, H, dk, dv), v.dtype), chunk_kv)
    q_w = jnp.exp(log_g[:, None] * (idx + 1.0)[None, :]).astype(v.dtype)
    cross = jnp.einsum('bhncd,nbhde->bhnce', qc, prev) * q_w[None, :, None, :, None]
    return (intra + cross).reshape(B, H, S, dv)


def setup_inputs(seed: int = 0) -> dict:
    key = jax.random.key(seed)
    ks = jax.random.split(key, 17)

    def nrm(k, shape, scale):
        return jax.random.normal(k, shape, jnp.float32) * scale

    base_decay = jnp.log(2.0 ** (5.0 + jnp.arange(RET_HEADS, dtype=jnp.float32)) - 1.0)
    return {
        'x': nrm(ks[0], (BATCH, SEQ, D_MODEL), 1.0),
        'norm1_g': 1.0 + nrm(ks[1], (DEPTH, D_MODEL), 0.02),
        'w_in': nrm(ks[2], (DEPTH, D_MODEL, IN_WIDTH), D_MODEL ** -0.5),
        'q_norm_g': 1.0 + nrm(ks[3], (DEPTH, DIFF_QK_DIM), 0.02),
        'k_norm_g': 1.0 + nrm(ks[4], (DEPTH, DIFF_QK_DIM), 0.02),
        'lambda_q1': nrm(ks[5], (DEPTH, DIFF_HEADS, DIFF_QK_DIM), 0.1),
        'lambda_k1': nrm(ks[6], (DEPTH, DIFF_HEADS, DIFF_QK_DIM), 0.1),
        'lambda_q2': nrm(ks[7], (DEPTH, DIFF_HEADS, DIFF_QK_DIM), 0.1),
        'lambda_k2': nrm(ks[8], (DEPTH, DIFF_HEADS, DIFF_QK_DIM), 0.1),
        'diff_out_g': 1.0 + nrm(ks[9], (DEPTH, DIFF_WIDTH), 0.02),
        'ret_decay_fwd': base_decay + nrm(ks[10], (DEPTH, RET_HEADS), 0.05),
        'ret_decay_bwd': base_decay + nrm(ks[11], (DEPTH, RET_HEADS), 0.05),
        'ret_gn_g': 1.0 + nrm(ks[12], (DEPTH, RET_WIDTH), 0.02),
        'w_out': nrm(ks[13], (DEPTH, MIX_WIDTH, D_MODEL), MIX_WIDTH ** -0.5),
        'norm2_g': 1.0 + nrm(ks[14], (DEPTH, D_MODEL), 0.02),
        'w_mlp1': nrm(ks[15], (DEPTH, D_MODEL, D_FF), D_MODEL ** -0.5),
        'w_mlp2': nrm(ks[16], (DEPTH, D_FF, D_MODEL), D_FF ** -0.5),
    }


def reference(x, norm1_g, w_in, q_norm_g, k_norm_g, lambda_q1, lambda_k1, lambda_q2, lambda_k2,
              diff_out_g, ret_decay_fwd, ret_decay_bwd, ret_gn_g, w_out, norm2_g, w_mlp1, w_mlp2):
    B, S, _ = x.shape
    slopes = jnp.power(2.0, -8.0 * jnp.arange(1, DIFF_HEADS + 1, dtype=jnp.float32) / DIFF_HEADS)
    offsets = np.cumsum(IN_SIZES)[:-1].tolist()
    for l in range(DEPTH):
        h = rms_norm(x, norm1_g[l])
        proj = h @ w_in[l]
        dq, dk, dv, rq, rk, rv, rg = jnp.split(proj, offsets, axis=-1)

        dq = rms_norm(dq.reshape(B, S, DIFF_HEADS, 2, DIFF_QK_DIM), q_norm_g[l]).transpose(0, 2, 3, 1, 4)
        dk = rms_norm(dk.reshape(B, S, DIFF_HEADS, 2, DIFF_QK_DIM), k_norm_g[l]).transpose(0, 2, 3, 1, 4)
        dv = dv.reshape(B, S, DIFF_HEADS, DIFF_V_DIM).transpose(0, 2, 1, 3)
        lam_init = lambda_init(l)
        lam = (jnp.exp(jnp.sum(lambda_q1[l].astype(jnp.float32) * lambda_k1[l].astype(jnp.float32), axis=-1))
               - jnp.exp(jnp.sum(lambda_q2[l].astype(jnp.float32) * lambda_k2[l].astype(jnp.float32), axis=-1))
               + lam_init)
        a = diff_attention(dq, dk, dv, lam, slopes)
        a = rms_norm(a, diff_out_g[l].reshape(DIFF_HEADS, DIFF_V_DIM)) * (1.0 - lam_init)
        a = a.reshape(B, S, DIFF_WIDTH)

        rq = rq.reshape(B, S, RET_HEADS, RET_QK_DIM).transpose(0, 2, 1, 3)
        rk = rk.reshape(B, S, RET_HEADS, RET_QK_DIM).transpose(0, 2, 1, 3) * (RET_QK_DIM ** -0.5)
        rv = rv.reshape(B, S, RET_HEADS, RET_V_DIM).transpose(0, 2, 1, 3)
        lg_f = jax.nn.log_sigmoid(ret_decay_fwd[l].astype(jnp.float32))
        lg_b = jax.nn.log_sigmoid(ret_decay_bwd[l].astype(jnp.float32))
        y_f = retention_scan(rq, rk, rv, lg_f, False)
        y_b = jnp.flip(retention_scan(jnp.flip(rq, 2), jnp.flip(rk, 2), jnp.flip(rv, 2), lg_b, True), 2)
        y = (y_f + y_b).transpose(0, 2, 1, 3)
        y = group_norm(y, ret_gn_g[l].reshape(RET_HEADS, RET_V_DIM)).reshape(B, S, RET_WIDTH)
        y = jax.nn.silu(rg) * y

        mix = jnp.concatenate([a, y], axis=-1)
        x = x + mix @ w_out[l]

        h = rms_norm(x, norm2_g[l])
        u = jax.nn.relu(h @ w_mlp1[l])
        x = x + (u * u) @ w_mlp2[l]
    return x
```

```python
import math
from contextlib import ExitStack
import numpy as np
import ml_dtypes
import concourse.bass as bass
import concourse.mybir as mybir
from concourse.bass_utils import run_bass_kernel_spmd

F32 = mybir.dt.float32
BF16 = mybir.dt.bfloat16
ALU = mybir.AluOpType
AF = mybir.ActivationFunctionType
AX = mybir.AxisListType
S = 4096
D = 1024
NT = 32
EPS = 1e-6
OFF = {}
_o = 0
for _nm, _n in [("n1", 1024), ("n2", 1024), ("qg", 64), ("kg", 64), ("lq1", 256), ("lk1", 256), ("lq2", 256),
                ("lk2", 256), ("dog", 512), ("rdf", 8), ("rdb", 8), ("gng", 512)]:
    OFF[_nm] = (_o, _n)
    _o += _n
NPV = _o
SLOPES = [2.0 ** (-8.0 * (h + 1) / 4) for h in range(4)]


class Buf:
    def __init__(self, t):
        self.t = t
        self.ready = []
        self.readers = []


class Slot:
    def __init__(self, K, name):
        self.K, self.name, self.sem, self.v = K, name, None, 0

    def dma(self, q, out, in_):
        K = self.K
        if self.sem is None or self.v + 16 > K.LIM:
            self.sem = K.newsem("d_" + self.name)
            self.v = 0
        ins = K.E[q].dma_start(out=out, in_=in_)
        ins.then_inc(self.sem, 16)
        self.v += 16
        return (self.sem, self.v)


class KB:
    LIM = 30000

    def __init__(self, nc, es):
        self.nc, self.es = nc, es
        self.E = dict(pe=nc.tensor, act=nc.scalar, dve=nc.vector, pool=nc.gpsimd, sp=nc.sync)
        self.cur, self.last, self.waited, self.nsem = {}, {}, {}, 0
        self.slots = {}

    def slot(self, name):
        if name not in self.slots:
            self.slots[name] = Slot(self, name)
        return self.slots[name]

    def newsem(self, name):
        self.nsem += 1
        return self.es.enter_context(self.nc.semaphore(f"{name}_{self.nsem}"))

    def sig(self, e, ins):
        c = self.cur.get(e)
        if c is None or c[1] + 1 > self.LIM:
            c = [self.newsem("c_" + e), 0]
            self.cur[e] = c
        ins.then_inc(c[0], 1)
        c[1] += 1
        ev = (c[0], c[1])
        self.last[e] = ev
        return ev

    def wait(self, e, ev):
        if ev is None:
            return
        sem, v = ev
        key = (e, sem.num)
        if self.waited.get(key, 0) >= v:
            return
        self.E[e].wait_ge(sem, v)
        self.waited[key] = v

    def W(self, e, *bufs):
        for b in bufs:
            for ev in b.readers + b.ready:
                self.wait(e, ev)

    def R(self, e, *bufs):
        for b in bufs:
            for ev in b.ready:
                self.wait(e, ev)

    def setw(self, ev, *bufs):
        for b in bufs:
            b.ready = [ev]
            b.readers = []

    def addw(self, ev, *bufs):
        for b in bufs:
            b.ready.append(ev)

    def setr(self, ev, *bufs):
        for b in bufs:
            b.readers.append(ev)

    def op(self, e, fn, outs=(), ins=(), partial=False):
        for b in ins:
            self.R(e, b)
        for b in outs:
            if partial:
                for ev in b.readers:
                    self.wait(e, ev)
            else:
                self.W(e, b)
        ev = self.sig(e, fn(self.E[e]))
        for b in outs:
            if partial:
                self.addw(ev, b)
            else:
                self.setw(ev, b)
        for b in ins:
            self.setr(ev, b)
        return ev

    def barrier(self, store_evs=()):
        for ev in store_evs:
            self.wait('pool', ev)
        self.sig('pool', self.E['pool'].memset(self.mark[:, 0:1], 0.0))
        for e in ('pe', 'act', 'dve', 'pool', 'sp'):
            for f in ('pe', 'act', 'dve', 'pool'):
                self.wait(e, self.last.get(f))


def lambda_init(layer):
    return 0.8 - 0.6 * math.exp(-0.3 * layer)


def rstd_chain(K, src, dst, lnb, n_inv, cols):
    K.op('dve', lambda e: e.tensor_scalar(out=lnb.t[:, 0:cols], in0=src.t[:, 0:cols], scalar1=n_inv, scalar2=EPS,
                                          op0=ALU.mult, op1=ALU.add), outs=[lnb], ins=[src])
    K.op('act', lambda e: e.activation(out=lnb.t[:, 0:cols], in_=lnb.t[:, 0:cols], func=AF.Ln), outs=[lnb], ins=[lnb])
    K.op('act', lambda e: e.activation(out=dst.t[:, 0:cols], in_=lnb.t[:, 0:cols], func=AF.Exp, scale=-0.5),
         outs=[dst], ins=[lnb])


def make_norm(K, gb, hbf, hT, tp, st, mv, ms, lnb, rs):
    nc = K.nc

    def chain_a(tg, tt):
        t = tg * 4 + tt
        i2 = t % 2
        xt = K.x[:, t, :]
        K.op('dve', lambda e: e.bn_stats(out=st[i2].t[:, 0:6], in_=xt[:, 0:512]), outs=[st[i2]])
        K.op('dve', lambda e: e.bn_stats(out=st[i2].t[:, 6:12], in_=xt[:, 512:1024]), outs=[st[i2]], partial=True)
        K.op('dve', lambda e: e.bn_aggr(out=mv[i2].t[:, 0:2], in_=st[i2].t[:, 0:12]), outs=[mv[i2]], ins=[st[i2]])
        K.op('dve', lambda e: e.scalar_tensor_tensor(out=ms[i2].t[:, 0:1], in0=mv[i2].t[:, 0:1], scalar=mv[i2].t[:, 0:1],
                                                     in1=mv[i2].t[:, 1:2], op0=ALU.mult, op1=ALU.add),
             outs=[ms[i2]], ins=[mv[i2]])
        K.op('dve', lambda e: e.tensor_scalar(out=lnb[i2].t[:, 0:1], in0=ms[i2].t[:, 0:1], scalar1=1.0, scalar2=EPS,
                                              op0=ALU.mult, op1=ALU.add), outs=[lnb[i2]], ins=[ms[i2]])

    def chain_b(tg, tt):
        t = tg * 4 + tt
        i2 = t % 2
        xt = K.x[:, t, :]
        K.op('act', lambda e: e.activation(out=lnb[i2].t[:, 0:1], in_=lnb[i2].t[:, 0:1], func=AF.Ln), outs=[lnb[i2]], ins=[lnb[i2]])
        K.op('act', lambda e: e.activation(out=rs[i2].t[:, 0:1], in_=lnb[i2].t[:, 0:1], func=AF.Exp, scale=-0.5),
             outs=[rs[i2]], ins=[lnb[i2]])
        K.op('dve', lambda e: e.scalar_tensor_tensor(out=hbf[i2].t[:], in0=xt, scalar=rs[i2].t[:, 0:1], in1=gb.t[:],
                                                     op0=ALU.mult, op1=ALU.mult), outs=[hbf[i2]], ins=[rs[i2], gb])

    def tr(tg, tt):
        t = tg * 4 + tt
        i2 = t % 2
        hb = hT[tg % 2]
        tpb = tp[i2]
        K.R('pe', hbf[i2])
        K.W('pe', tpb)
        for k in range(8):
            ins = nc.tensor.transpose(out=tpb.t[:, k, :], in_=hbf[i2].t[:, k * 128:(k + 1) * 128], identity=K.ident[:])
        ev = K.sig('pe', ins)
        K.setw(ev, tpb)
        K.setr(ev, hbf[i2])
        K.op('act', lambda e: e.copy(out=hb.t[:, :, tt * 128:(tt + 1) * 128], in_=tpb.t[:]), outs=[hb], ins=[tpb],
             partial=(tt > 0))
    return chain_a, chain_b, tr


def build_program(L, lam_layers, debug=False, phases="ABCDE"):
    nc = bass.Bass("TRN2", target_bir_lowering=False)
    dk = "ExternalOutput" if debug else "Internal"
    T = {}
    T['x'] = nc.dram_tensor("x", [S, D], F32, kind="ExternalInput").ap()
    T['y'] = nc.dram_tensor("y", [S, D], F32, kind="ExternalOutput").ap()
    T['w_in'] = nc.dram_tensor("w_in", [L, D, 3584], F32, kind="ExternalInput").ap()
    T['w_out'] = nc.dram_tensor("w_out", [L, D, D], F32, kind="ExternalInput").ap()
    T['w1'] = nc.dram_tensor("w1", [L, D, 4096], F32, kind="ExternalInput").ap()
    T['w2'] = nc.dram_tensor("w2", [L, 4096, D], F32, kind="ExternalInput").ap()
    T['pv'] = nc.dram_tensor("pv", [L, NPV], F32, kind="ExternalInput").ap()
    T['qaug'] = nc.dram_tensor("qaug", [2, 8, S], BF16, kind="ExternalInput").ap()
    T['kaug'] = nc.dram_tensor("kaug", [4, 8, S], BF16, kind="ExternalInput").ap()
    T['bd'] = nc.dram_tensor("bd", [128, 4, 128], BF16, kind="ExternalInput").ap()
    T['rc'] = nc.dram_tensor("rc", [128, 5 * 128 + 2], F32, kind="ExternalInput").ap()
    T['wb_in'] = nc.dram_tensor("wb_in", [L, D, 3584], BF16, kind="Internal").ap()
    T['wb_out'] = nc.dram_tensor("wb_out", [L, D, D], BF16, kind="Internal").ap()
    T['wb1'] = nc.dram_tensor("wb1", [L, D, 4096], BF16, kind="Internal").ap()
    T['wb2'] = nc.dram_tensor("wb2", [L, 4096, D], BF16, kind="Internal").ap()
    T['QT'] = nc.dram_tensor("QT", [4, 2, 64, S], BF16, kind=dk).ap()
    T['KT'] = nc.dram_tensor("KT", [4, 2, 64, S], BF16, kind=dk).ap()
    T['V'] = nc.dram_tensor("V", [S, 512], BF16, kind=dk).ap()
    T['RQT'] = nc.dram_tensor("RQT", [8, 64, S], BF16, kind=dk).ap()
    T['RKT'] = nc.dram_tensor("RKT", [8, 64, S], BF16, kind=dk).ap()
    T['RK'] = nc.dram_tensor("RK", [S, 512], BF16, kind=dk).ap()
    T['RV'] = nc.dram_tensor("RV", [S, 512], BF16, kind=dk).ap()
    T['GATE'] = nc.dram_tensor("GATE", [S, 512], F32, kind=dk).ap()
    T['MIX'] = nc.dram_tensor("MIX", [S, D], BF16, kind=dk).ap()

    with ExitStack() as es:
        K = KB(nc, es)
        K.T = T
        K.mark = es.enter_context(nc.sbuf_tensor("mark", [128, 4], F32))
        xs = es.enter_context(nc.sbuf_tensor("xres", [128, NT, D], F32))
        K.x = xs
        ident = es.enter_context(nc.sbuf_tensor("ident", [128, 128], BF16))
        K.ident = ident
        cast_ev = []
        for l in range(L):
            d = {}
            for nm, wnm, rows in (("in", "w_in", D), ("out", "w_out", D), ("1", "w1", D), ("2", "w2", 4096)):
                sl = Slot(K, f"cast{l}{nm}")
                ev = None
                for r in range(rows // 128 if "Z" not in phases else 0):
                    ev = sl.dma('pool', T['wb' + ('_' + nm if nm in ('in', 'out') else nm)][l, r * 128:(r + 1) * 128, :],
                                T[wnm][l, r * 128:(r + 1) * 128, :])
                d[nm] = ev
            cast_ev.append(d if "Z" not in phases else dict.fromkeys(['in','out','1','2']))
        K.wait('pool', K.sig('pool', nc.gpsimd.memset(ident[:], 0.0)))
        K.ident_ev = K.sig('pool', nc.gpsimd.affine_select(out=ident[:], in_=ident[:], pattern=[[-1, 128]],
                                                           compare_op=ALU.not_equal, fill=1.0, base=0,
                                                           channel_multiplier=1))
        xsl = Slot(K, "xld")
        xv = T['x'].rearrange("(t p) d -> p t d", p=128)
        for i in range(8):
            xev = xsl.dma('sp', xs[:, i * 4:(i + 1) * 4, :], xv[:, i * 4:(i + 1) * 4, :])
        for e in ('pe', 'act', 'dve', 'pool'):
            K.wait(e, xev)
            K.wait(e, K.ident_ev)
        for l in range(L):
            li = lam_layers[l]
            if "A" in phases:
                phase_A(K, l, cast_ev[l]['in'])
            if "B" in phases:
                phase_B(K, l, li)
            if "C" in phases:
                phase_C(K, l)
            if "D" in phases:
                phase_D(K, l, cast_ev[l]['out'])
            if "E" in phases:
                phase_E(K, l, cast_ev[l]['1'], cast_ev[l]['2'])
        for d in cast_ev:
            for ev in d.values():
                K.wait('sp', ev)
        osl = Slot(K, "xst")
        yv = T['y'].rearrange("(t p) d -> p t d", p=128)
        for i in range(8):
            oev = osl.dma('sp', yv[:, i * 4:(i + 1) * 4, :], xs[:, i * 4:(i + 1) * 4, :])
        nc.sync.wait_ge(oev[0], oev[1])
    return nc


def pvv(K, l, nm):
    o, n = OFF[nm]
    return K.T['pv'][l, o:o + n]


def phase_A(K, l, cast_ev):
    nc, T = K.nc, K.T
    with ExitStack() as es:
        def sb(n, shp, dt):
            return Buf(es.enter_context(nc.sbuf_tensor(f"A{l}_{n}", shp, dt)))

        def psum(n, shp, dt):
            return Buf(es.enter_context(nc.psum_tensor(f"A{l}_{n}", shp, dt)))
        g1b = sb("g1b", [128, 1024], F32)
        gqk = sb("gqk", [128, 2, 64], F32)
        psl = K.slot(f"Apar")
        psl.dma('sp', g1b.t[:], pvv(K, l, "n1").partition_broadcast(128))
        psl.dma('sp', gqk.t[:, 0, :], pvv(K, l, "qg").partition_broadcast(128))
        pev = psl.dma('sp', gqk.t[:, 1, :], pvv(K, l, "kg").partition_broadcast(128))
        K.setw(pev, g1b, gqk)
        K.op('dve', lambda e: e.tensor_scalar(out=gqk.t[:, 0, :], in0=gqk.t[:, 0, :], scalar1=0.125, scalar2=None,
                                              op0=ALU.mult), outs=[gqk], ins=[gqk])
        hbf = [sb(f"hbf{i}", [128, 1024], BF16) for i in range(2)]
        hT = [sb(f"hT{i}", [128, 8, 512], BF16) for i in range(2)]
        Wb = [sb(f"W{i}", [128, 8, 512], BF16) for i in range(2)]
        Wsl = [K.slot(f"AW{i}") for i in range(2)]
        st = [sb(f"st{i}", [128, 12], F32) for i in range(2)]
        mv = [sb(f"mv{i}", [128, 2], F32) for i in range(2)]
        ms = [sb(f"ms{i}", [128, 1], F32) for i in range(2)]
        lnb = [sb(f"lnb{i}", [128, 8], F32) for i in range(2)]
        rs = [sb(f"rs{i}", [128, 8], F32) for i in range(2)]
        qsb = [sb(f"qsb{i}", [128, 512], F32) for i in range(2)]
        sqb = sb("sqb", [128, 512], F32)
        ssq = [sb(f"ssq{i}", [128, 8], F32) for i in range(2)]
        tmpb = sb("tmpb", [128, 512], F32)
        lnq = [sb(f"lnq{i}", [128, 8], F32) for i in range(2)]
        rsq = [sb(f"rsq{i}", [128, 8], F32) for i in range(2)]
        qhat = [sb(f"qhat{i}", [128, 512], BF16) for i in range(4)]
        trst = {nm: sb(f"tr_{nm}", [128, 4, 512], BF16) for nm in ("q", "k")}
        trst["rq"], trst["rk"] = trst["q"], trst["k"]
        trsl = {nm: K.slot(f"Atr{nm}") for nm in ("q", "k", "rq", "rk")}
        tmst = {nm: [sb(f"tm_{nm}{i}", [128, 512], BF16) for i in range(4 if nm == "rk" else 3)] for nm in ("v", "rk", "rv")}
        gst = [sb(f"gst{i}", [128, 512], F32) for i in range(3)]
        tmsl = {nm: [K.slot(f"Atm{nm}{i}") for i in range(4 if nm == "rk" else 3)] for nm in ("v", "rk", "rv", "g")}
        tp = [psum(f"tp{i}", [128, 8, 128], BF16) for i in range(2)]
        mm = [psum(f"mm{i}", [128, 512], F32) for i in range(3)]
        tq = [psum(f"tq{i}", [128, 4, 128], BF16) for i in range(2)]
        cnt = dict(mm=0, tq=0, q=0, tm=0)
        store_evs = []
        chunks = [(tg, ct) for tg in range(8) for ct in range(7)]
        wv = T['wb_in'][l].rearrange("(k p) c -> p k c", p=128)

        def issue(i):
            tg, ct = chunks[i]
            b = Wb[i % 2]
            K.W('sp', b)
            if i == 0:
                K.wait('sp', cast_ev)
            ev = Wsl[i % 2].dma('sp', b.t[:], wv[:, :, ct * 512:(ct + 1) * 512])
            K.setw(ev, b)

        chain_a, chain_b, tr = make_norm(K, g1b, hbf, hT, tp, st, mv, ms, lnb, rs)

        pending_pe = []
        pending_q2 = []

        def transposes_to(src, nm, tt, tg):
            def run():
                tqb = tq[cnt['tq'] % 2]
                cnt['tq'] += 1
                K.R('pe', src)
                K.W('pe', tqb)
                for j in range(4):
                    ins = nc.tensor.transpose(out=tqb.t[:, j, :], in_=src.t[:, j * 128:(j + 1) * 128], identity=K.ident[:])
                ev = K.sig('pe', ins)
                K.setw(ev, tqb)
                K.setr(ev, src)
                stg = trst[nm]
                K.op('dve', lambda e: e.tensor_copy(out=stg.t[:, :, tt * 128:(tt + 1) * 128], in_=tqb.t[:]), outs=[stg],
                     ins=[tqb], partial=(tt > 0))
                if tt == 3:
                    K.R('pool', stg)
                    tok = slice(tg * 512, (tg + 1) * 512)
                    if nm == "q":
                        dst = T['QT'].rearrange("h c d s -> (c d) h s")[:, :, tok]
                    elif nm == "k":
                        dst = T['KT'].rearrange("h c d s -> (c d) h s")[:, :, tok]
                    elif nm == "rq":
                        dst = T['RQT'].rearrange("(p two) d s -> (two d) p s", two=2)[:, :, tok]
                    else:
                        dst = T['RKT'].rearrange("(p two) d s -> (two d) p s", two=2)[:, :, tok]
                    ev2 = trsl[nm].dma('pool', dst, stg.t[:])
                    K.setr(ev2, stg)
                    store_evs.append(ev2)
            return run

        def tm_store(nm, buf, slot, dram, t):
            K.R('pool', buf)
            ev = slot.dma('pool', dram[t * 128:(t + 1) * 128, :], buf.t[:])
            K.setr(ev, buf)
            store_evs.append(ev)

        def evac(ct, tg, tt, m):
            t = tg * 4 + tt
            if ct in (0, 1):
                i2 = cnt['q'] % 2
                i4 = cnt['q'] % 4
                cnt['q'] += 1
                K.op('act', lambda e: e.copy(out=qsb[i2].t[:], in_=m.t[:]), outs=[qsb[i2]], ins=[m])
                K.op('act', lambda e: e.activation(out=sqb.t[:], in_=m.t[:], func=AF.Square), outs=[sqb], ins=[m])
                K.op('dve', lambda e: e.tensor_reduce(out=ssq[i2].t[:], in_=sqb.t[:].rearrange("p (g d) -> p g d", g=8),
                                                      axis=AX.X, op=ALU.add), outs=[ssq[i2]], ins=[sqb])
                K.op('dve', lambda e: e.tensor_scalar(out=lnq[i2].t[:, 0:8], in0=ssq[i2].t[:, 0:8], scalar1=1.0 / 64, scalar2=EPS,
                                                      op0=ALU.mult, op1=ALU.add), outs=[lnq[i2]], ins=[ssq[i2]])

                def part2(i2=i2, i4=i4, ct=ct, tt=tt, tg=tg):
                    K.op('act', lambda e: e.activation(out=lnq[i2].t[:, 0:8], in_=lnq[i2].t[:, 0:8], func=AF.Ln), outs=[lnq[i2]], ins=[lnq[i2]])
                    K.op('act', lambda e: e.activation(out=rsq[i2].t[:, 0:8], in_=lnq[i2].t[:, 0:8], func=AF.Exp, scale=-0.5),
                         outs=[rsq[i2]], ins=[lnq[i2]])
                    K.op('dve', lambda e: e.tensor_tensor(out=tmpb.t[:].rearrange("p (g d) -> p g d", g=8),
                                                          in0=qsb[i2].t[:].rearrange("p (g d) -> p g d", g=8),
                                                          in1=rsq[i2].t[:, 0:8].unsqueeze(2).to_broadcast([128, 8, 64]), op=ALU.mult),
                         outs=[tmpb], ins=[qsb[i2], rsq[i2]])
                    K.op('dve', lambda e: e.tensor_tensor(out=qhat[i4].t[:].rearrange("p (g d) -> p g d", g=8),
                                                          in0=tmpb.t[:].rearrange("p (g d) -> p g d", g=8),
                                                          in1=gqk.t[:, ct, :].unsqueeze(1).to_broadcast([128, 8, 64]), op=ALU.mult),
                         outs=[qhat[i4]], ins=[tmpb, gqk])
                    pending_pe.append(transposes_to(qhat[i4], "q" if ct == 0 else "k", tt, tg))
                pending_q2.append(part2)
            elif ct in (2, 5):
                nm = "v" if ct == 2 else "rv"
                b = tmst[nm][t % 3]
                K.op('act', lambda e: e.copy(out=b.t[:], in_=m.t[:]), outs=[b], ins=[m])
                tm_store(nm, b, tmsl[nm][t % 3], T['V'] if ct == 2 else T['RV'], t)
            elif ct == 3:
                i4 = cnt['q'] % 4
                cnt['q'] += 1
                K.op('act', lambda e: e.copy(out=qhat[i4].t[:], in_=m.t[:]), outs=[qhat[i4]], ins=[m])
                pending_pe.append(transposes_to(qhat[i4], "rq", tt, tg))
            elif ct == 4:
                b = tmst["rk"][t % 4]
                K.op('act', lambda e: e.activation(out=b.t[:], in_=m.t[:], func=AF.Copy, scale=0.125), outs=[b], ins=[m])
                tm_store("rk", b, tmsl["rk"][t % 4], T['RK'], t)
                pending_pe.append(transposes_to(b, "rk", tt, tg))
            else:
                b = gst[t % 3]
                K.op('act', lambda e: e.activation(out=b.t[:], in_=m.t[:], func=AF.Silu), outs=[b], ins=[m])
                tm_store("g", b, tmsl["g"][t % 3], T['GATE'], t)

        issue(0)
        for t0_ in (0, 2):
            chain_a(0, t0_)
            chain_a(0, t0_ + 1)
            chain_b(0, t0_)
            chain_b(0, t0_ + 1)
            tr(0, t0_)
            tr(0, t0_ + 1)
        for i, (tg, ct) in enumerate(chunks):
            if i + 1 < len(chunks):
                issue(i + 1)
            nxt = tg + 1 < 8
            if nxt and 1 <= ct <= 4:
                chain_a(tg + 1, ct - 1)
            wb = Wb[i % 2]
            hb = hT[tg % 2]
            for tt in range(4):
                if tt == 2 and nxt and 1 <= ct <= 4:
                    chain_b(tg + 1, ct - 1)
                m = mm[cnt['mm'] % 3]
                cnt['mm'] += 1
                K.R('pe', wb, hb)
                K.W('pe', m)
                for k in range(8):
                    ins = nc.tensor.matmul(m.t[:], lhsT=hb.t[:, k, tt * 128:(tt + 1) * 128], rhs=wb.t[:, k, :],
                                           start=(k == 0), stop=(k == 7))
                ev = K.sig('pe', ins)
                K.setw(ev, m)
                K.setr(ev, wb, hb)
                todo2 = list(pending_q2)
                pending_q2.clear()
                evac(ct, tg, tt, m)
                for f in todo2:
                    f()
                while len(pending_pe) > 2:
                    pending_pe.pop(0)()
            if nxt and 2 <= ct <= 5:
                tr(tg + 1, ct - 2)
        for f in pending_q2:
            f()
        while pending_pe:
            pending_pe.pop(0)()
        K.barrier(store_evs)


BAND_THR = 64.0
EXP_SHIFT = -8.0


def kept_blocks(h, qt):
    out = []
    for kb in range(32):
        lo, hi = kb * 128, kb * 128 + 127
        if lo > qt * 512 + 511:
            dmin = lo - (qt * 512 + 511)
        elif hi < qt * 512:
            dmin = qt * 512 - hi
        else:
            dmin = 0
        if SLOPES[h] * dmin < BAND_THR:
            out.append(kb)
    return out


def phase_B(K, l, li):
    nc, T = K.nc, K.T
    lam0 = lambda_init(li)
    with ExitStack() as es:
        def sb(n, shp, dt, st=None):
            return Buf((st or es).enter_context(nc.sbuf_tensor(f"B{l}_{n}", shp, dt)))

        def psum(n, shp, dt):
            return Buf(es.enter_context(nc.psum_tensor(f"B{l}_{n}", shp, dt)))
        lms = sb("lms", [128, 8], F32)
        lam = sb("lam", [128, 4], F32)
        dog = sb("dog", [128, 512], F32)
        bd = sb("bd", [128, 4, 128], BF16)
        sbias = sb("sbias", [128, 1], F32)
        K.op('pool', lambda e: e.memset(sbias.t[:], EXP_SHIFT), outs=[sbias])
        psl = K.slot("Bpar")
        with ExitStack() as es0:
            lmb = sb("lmb", [128, 4, 256], F32, es0)
            lmp = sb("lmp", [128, 2, 256], F32, es0)
            for j, nm in enumerate(("lq1", "lk1", "lq2", "lk2")):
                psl.dma('sp', lmb.t[:, j, :], pvv(K, l, nm).partition_broadcast(128))
            psl.dma('sp', dog.t[:], pvv(K, l, "dog").partition_broadcast(128))
            pev = psl.dma('sp', bd.t[:], T['bd'])
            K.setw(pev, lmb, dog, bd)
            K.op('dve', lambda e: e.tensor_tensor(out=lmp.t[:, 0, :], in0=lmb.t[:, 0, :], in1=lmb.t[:, 1, :], op=ALU.mult),
                 outs=[lmp], ins=[lmb])
            K.op('dve', lambda e: e.tensor_tensor(out=lmp.t[:, 1, :], in0=lmb.t[:, 2, :], in1=lmb.t[:, 3, :], op=ALU.mult),
                 outs=[lmp], ins=[lmb], partial=True)
            K.op('dve', lambda e: e.tensor_reduce(out=lms.t[:, 0:8], in_=lmp.t[:].rearrange("p a (h d) -> p (a h) d", h=4),
                                                  axis=AX.X, op=ALU.add), outs=[lms], ins=[lmp])
            K.op('act', lambda e: e.activation(out=lms.t[:, 0:8], in_=lms.t[:, 0:8], func=AF.Exp), outs=[lms], ins=[lms])
            K.op('dve', lambda e: e.tensor_tensor(out=lam.t[:, 0:4], in0=lms.t[:, 0:4], in1=lms.t[:, 4:8], op=ALU.subtract),
                 outs=[lam], ins=[lms])
            K.op('dve', lambda e: e.tensor_scalar(out=lam.t[:, 0:4], in0=lam.t[:, 0:4], scalar1=lam0, scalar2=None, op0=ALU.add),
                 outs=[lam], ins=[lam])
            K.op('dve', lambda e: e.tensor_scalar(out=dog.t[:], in0=dog.t[:], scalar1=1.0 - lam0, scalar2=None, op0=ALU.mult),
                 outs=[dog], ins=[dog])
            for e_ in ('pe', 'act', 'dve', 'pool', 'sp'):
                K.R(e_, lam, dog, bd)
        kT = [sb(f"kT{i}", [72, 2, S], BF16) for i in range(2)]
        vS = [sb(f"vS{i}", [128, 32, 132], BF16) for i in range(2)]
        qS = [sb(f"qS{i}", [72, 2, 2, 512], BF16) for i in range(2)]
        NPT = 8
        PT = [sb(f"pt{i}", [128, 512], BF16) for i in range(NPT)]
        osb_tt = [es.enter_context(nc.sbuf_tensor(f"B{l}_osb{i}", [128, 8, 132], F32)) for i in range(2)]
        osb_b = [[Buf(osb_tt[i]), Buf(osb_tt[i])] for i in range(2)]
        rsm_b = [sb(f"rsm{i}", [128, 8], F32) for i in range(2)]
        t2 = sb("t2", [128, 128], F32)
        ab = sb("ab", [128, 128], F32)
        jk = sb("jk", [128, 128], F32)
        ssb = sb("ssb", [128, 8], F32)
        lnb = sb("lnb", [128, 8], F32)
        rstd = sb("rstd", [128, 8], F32)
        mst = [sb(f"mst{i}", [128, 4, 128], BF16) for i in range(2)]
        msl = [K.slot(f"Bm{i}") for i in range(2)]
        ksl = [K.slot(f"Bk{i}") for i in range(2)]
        qsl = [K.slot(f"Bq{i}") for i in range(2)]
        Sp = [psum(f"s{i}", [128, 512], F32) for i in range(4)]
        Op_t = es.enter_context(nc.psum_tensor(f"B{l}_o", [128, 8, 256], F32))
        Op = [Buf(Op_t), Buf(Op_t)]
        store_evs = []
        ones_ev = []
        for i in range(2):
            K.op('pool', lambda e: e.memset(vS[i].t[:, :, 128:132], 1.0), outs=[vS[i]])
            ones_ev.append(vS[i].ready[0])
        LA = 4
        nq = 0

        def load_kv(h):
            kb_, vb_, sl = kT[h % 2], vS[h % 2], ksl[h % 2]
            K.W('sp', kb_, vb_)
            for c in range(2):
                sl.dma('sp', kb_.t[0:64, c, :], T['KT'][h, c])
                sl.dma('sp', kb_.t[64:72, c, :], T['kaug'][h])
            kev = sl.dma('sp', vb_.t[:, :, 0:128], T['V'][:, h * 128:(h + 1) * 128].rearrange("(n p) d -> p n d", p=128))
            K.setw(kev, kb_)
            vb_.ready = [kev, ones_ev[h % 2]]
            vb_.readers = []

        def load_q(h, qt):
            b = qS[(h * 8 + qt) % 2]
            sl = qsl[(h * 8 + qt) % 2]
            K.W('sp', b)
            tok = slice(qt * 512, (qt + 1) * 512)
            for var in range(2):
                for c in range(2):
                    sl.dma('sp', b.t[0:64, var, c, :], T['QT'][h, c, :, tok])
                    ev = sl.dma('sp', b.t[64:72, var, c, :], T['qaug'][var, :, tok])
            K.setw(ev, b)

        tiles = []
        for h in range(4):
            for qt in range(8):
                kbs = kept_blocks(h, qt)
                for c in range(2):
                    for kb in kbs:
                        tiles.append((h, qt, c, kb, kb == kbs[0], kb == kbs[-1]))
        n = len(tiles)
        load_kv(0)
        load_q(0, 0)

        def qk(i):
            h, qt, c, kb, first, last = tiles[i]
            if c == 0 and first:
                if qt == 1 and h + 1 < 4:
                    load_kv(h + 1)
                nh, nqt = (h, qt + 1) if qt + 1 < 8 else (h + 1, 0)
                if nh < 4:
                    load_q(nh, nqt)
            kTb = kT[h % 2]
            qb_ = qS[(h * 8 + qt) % 2]
            sp = Sp[i % 4]
            K.R('pe', kTb, qb_)
            K.W('pe', sp)
            kcols = slice(kb * 128, (kb + 1) * 128)
            q0 = qt * 4
            if kb < q0 or kb >= q0 + 4:
                var = 0 if kb < q0 else 1
                ins = nc.tensor.matmul(sp.t[:], lhsT=kTb.t[0:72, c, kcols], rhs=qb_.t[0:72, var, c, :], start=True, stop=True)
            else:
                for qb in range(4):
                    cols = slice(qb * 128, (qb + 1) * 128)
                    gq = q0 + qb
                    if gq != kb:
                        var = 0 if gq > kb else 1
                        ins = nc.tensor.matmul(sp.t[:, cols], lhsT=kTb.t[0:72, c, kcols], rhs=qb_.t[0:72, var, c, cols],
                                               start=True, stop=True)
                    else:
                        nc.tensor.matmul(sp.t[:, cols], lhsT=kTb.t[0:64, c, kcols], rhs=qb_.t[0:64, 0, c, cols],
                                         start=True, stop=False)
                        ins = nc.tensor.matmul(sp.t[:, cols], lhsT=K.ident[:, :], rhs=bd.t[:, h, :], start=False, stop=True)
            ev = K.sig('pe', ins)
            K.setw(ev, sp)
            K.setr(ev, kTb, qb_)
            pt = PT[i % NPT]
            K.op('act', lambda e: e.activation(out=pt.t[:], in_=sp.t[:], func=AF.Exp, bias=sbias.t[:, 0:1]), outs=[pt], ins=[sp, sbias])

        def pv(i):
            h, qt, c, kb, first, last = tiles[i]
            pt = PT[i % NPT]
            vb_ = vS[h % 2]
            K.R('pe', pt, vb_)
            if first:
                K.W('pe', Op[c])
            for qb in range(4):
                slot = c * 4 + qb
                ins = nc.tensor.matmul(Op_t[:, slot, 0:129], lhsT=pt.t[:, qb * 128:(qb + 1) * 128], rhs=vb_.t[:, kb, 0:129],
                                       start=(first and qb % 2 == 0), stop=last, skip_group_check=True)
            ev = K.sig('pe', ins)
            K.setr(ev, pt, vb_)
            if last:
                K.setw(ev, Op[c])
                osb_t, osb = osb_tt[(h * 8 + qt) % 2], osb_b[(h * 8 + qt) % 2]
                K.op('dve', lambda e: e.tensor_copy(out=osb_t[:, c * 4:(c + 1) * 4, 0:129], in_=Op_t[:, c * 4:(c + 1) * 4, 0:129]),
                     outs=[osb[c]], ins=[Op[c]])
                if c == 1:
                    while deferred:
                        next(deferred.pop(0)[1])
                    st = finalize(qt, h)
                    next(st)
                    deferred.append((i + 12, st))

        def finalize(qt, h):
            nonlocal nq
            osb_t, osb, rsm = osb_tt[(h * 8 + qt) % 2], osb_b[(h * 8 + qt) % 2], rsm_b[(h * 8 + qt) % 2]
            K.op('dve', lambda e: e.reciprocal(out=rsm.t[:, 0:8], in_=osb_t[:, :, 128]), outs=[rsm], ins=[osb[0], osb[1]])
            K.op('dve', lambda e: e.tensor_scalar(out=rsm.t[:, 4:8], in0=rsm.t[:, 4:8], scalar1=lam.t[:, h:h + 1], scalar2=None,
                                                  op0=ALU.mult), outs=[rsm], ins=[rsm, lam])
            mb = mst[nq % 2]
            msl_ = msl[nq % 2]
            nq += 1
            for ps_ in range(2):
                for qb in range(4):
                    K.op('dve', lambda e: e.tensor_scalar(out=t2.t[:], in0=osb_t[:, 4 + qb, 0:128], scalar1=rsm.t[:, 4 + qb:5 + qb],
                                                          scalar2=None, op0=ALU.mult), outs=[t2], ins=[osb[1], rsm])
                    K.op('dve', lambda e: e.scalar_tensor_tensor(out=ab.t[:], in0=osb_t[:, qb, 0:128], scalar=rsm.t[:, qb:qb + 1],
                                                                 in1=t2.t[:], op0=ALU.mult, op1=ALU.subtract),
                         outs=[ab], ins=[osb[0], rsm, t2])
                    if ps_ == 0:
                        K.op('dve', lambda e: e.tensor_tensor(out=jk.t[:], in0=ab.t[:], in1=ab.t[:], op=ALU.mult), outs=[jk], ins=[ab])
                        K.op('dve', lambda e: e.tensor_reduce(out=ssb.t[:, qb:qb + 1], in_=jk.t[:], axis=AX.X, op=ALU.add),
                             outs=[ssb], ins=[jk], partial=(qb > 0))
                    else:
                        K.op('dve', lambda e: e.scalar_tensor_tensor(out=mb.t[:, qb, :], in0=ab.t[:], scalar=rstd.t[:, qb:qb + 1],
                                                                     in1=dog.t[:, h * 128:(h + 1) * 128], op0=ALU.mult, op1=ALU.mult),
                             outs=[mb], ins=[ab, rstd, dog], partial=(qb > 0))
                if ps_ == 0:
                    K.op('dve', lambda e: e.tensor_scalar(out=lnb.t[:, 0:4], in0=ssb.t[:, 0:4], scalar1=1.0 / 128, scalar2=EPS,
                                                          op0=ALU.mult, op1=ALU.add), outs=[lnb], ins=[ssb])
                    yield
                    K.op('act', lambda e: e.activation(out=lnb.t[:, 0:4], in_=lnb.t[:, 0:4], func=AF.Ln), outs=[lnb], ins=[lnb])
                    K.op('act', lambda e: e.activation(out=rstd.t[:, 0:4], in_=lnb.t[:, 0:4], func=AF.Exp, scale=-0.5),
                         outs=[rstd], ins=[lnb])
            K.R('pool', mb)
            dst = T['MIX'][qt * 512:(qt + 1) * 512, h * 128:(h + 1) * 128].rearrange("(q p) d -> p q d", p=128)
            ev = msl_.dma('pool', dst, mb.t[:])
            K.setr(ev, mb)
            store_evs.append(ev)
            yield

        deferred = []
        for i in range(n + LA):
            if i < n:
                qk(i)
            if i - LA >= 0:
                pv(i - LA)
                while deferred and deferred[0][0] <= i - LA:
                    next(deferred.pop(0)[1])
        while deferred:
            next(deferred.pop(0)[1])
        K.barrier(store_evs)


import os
CSTOP = int(os.environ.get('CSTOP', '99'))


def phase_C(K, l):
    nc, T = K.nc, K.T
    with ExitStack() as es:
        def sb(n, shp, dt):
            return Buf(es.enter_context(nc.sbuf_tensor(f"C{l}_{n}", shp, dt)))

        def psum(n, shp, dt):
            return Buf(es.enter_context(nc.psum_tensor(f"C{l}_{n}", shp, dt)))
        rc = sb("rc", [128, 5 * 128 + 2], F32)
        dec = sb("dec", [128, 16], F32)
        gng = sb("gng", [128, 512], F32)
        psl = K.slot(f"Cpar")
        psl.dma('sp', rc.t[:], T['rc'])
        psl.dma('sp', dec.t[:, 0:8], pvv(K, l, "rdf").partition_broadcast(128))
        psl.dma('sp', dec.t[:, 8:16], pvv(K, l, "rdb").partition_broadcast(128))
        pev = psl.dma('sp', gng.t[:], pvv(K, l, "gng").partition_broadcast(128))
        K.setw(pev, rc, dec, gng)
        lg = sb("lg", [128, 16], F32)
        lgsel = sb("lgsel", [128, 8], F32)
        gC = sb("gC", [128, 8], F32)
        arg = sb("arg", [128, 8, 2], F32)
        scK = sb("scK", [128, 8, 2], F32)
        K.op('act', lambda e: e.activation(out=lg.t[:], in_=dec.t[:], func=AF.Exp, scale=-1.0), outs=[lg], ins=[dec])
        K.op('dve', lambda e: e.tensor_scalar(out=lg.t[:], in0=lg.t[:], scalar1=1.0, scalar2=None, op0=ALU.add), outs=[lg], ins=[lg])
        K.op('act', lambda e: e.activation(out=lg.t[:], in_=lg.t[:], func=AF.Ln), outs=[lg], ins=[lg])
        K.op('dve', lambda e: e.tensor_scalar(out=lg.t[:], in0=lg.t[:], scalar1=-1.0, scalar2=None, op0=ALU.mult), outs=[lg], ins=[lg])
        K.op('dve', lambda e: e.tensor_copy(out=lgsel.t[0:64, :], in_=lg.t[0:64, 0:8]), outs=[lgsel], ins=[lg])
        K.op('dve', lambda e: e.tensor_copy(out=lgsel.t[64:128, :], in_=lg.t[64:128, 8:16]), outs=[lgsel], ins=[lg], partial=True)
        K.op('act', lambda e: e.activation(out=gC.t[:], in_=lgsel.t[:], func=AF.Exp, scale=128.0), outs=[gC], ins=[lgsel])
        cK = 5 * 128
        K.op('dve', lambda e: e.tensor_scalar(out=arg.t[:, :, 0], in0=lg.t[:, 0:8], scalar1=rc.t[:, cK:cK + 1], scalar2=None,
                                              op0=ALU.mult), outs=[arg], ins=[lg, rc])
        K.op('dve', lambda e: e.tensor_scalar(out=arg.t[:, :, 1], in0=lg.t[:, 8:16], scalar1=rc.t[:, cK + 1:cK + 2], scalar2=None,
                                              op0=ALU.mult), outs=[arg], ins=[lg, rc], partial=True)
        K.op('act', lambda e: e.activation(out=scK.t[:], in_=arg.t[:], func=AF.Exp), outs=[scK], ins=[arg])
        MT = sb("MT", [128, 128], F32)
        e1 = sb("e1", [128, 128], F32)
        e2 = sb("e2", [128, 128], F32)
        G = sb("G", [128, 128], BF16)
        gm = sb("gm", [128, 8, 32], F32)
        K.op('dve', lambda e: e.memset(gm.t[:], 0.0), outs=[gm])
        qT2 = sb("qT2", [128, S], BF16)
        kTr = sb("kTr", [64, S], BF16)
        qfb = sb("qfb", [128, S], BF16)
        ktm = sb("ktm", [128, 32, 64], BF16)
        kfb = sb("kfb", [128, 32, 128], BF16)
        vSb = [sb(f"vS{i}", [128, 32, 64], BF16) for i in range(2)]
        kvT = sb("kvT", [128, 64, 32], F32)
        Spb = [sb(f"Sperm{i}", [128, 32, 64], BF16) for i in range(2)]
        for Sp0 in Spb:
            K.op('pool', lambda e: e.memset(Sp0.t[0:64, 0, :], 0.0), outs=[Sp0])
            K.op('pool', lambda e: e.memset(Sp0.t[64:128, 31, :], 0.0), outs=[Sp0], partial=True)
        gate = [sb(f"gate{i}", [128, 8, 64], F32) for i in range(2)]
        A_sb = [sb(f"A{i}", [128, 128], BF16) for i in range(4)]
        ysb = sb("ysb", [128, 8, 64], F32)
        ysq = sb("ysq", [128, 8, 64], F32)
        s1 = sb("s1", [128, 8], F32)
        s2 = sb("s2", [128, 8], F32)
        var = sb("var", [128, 8], F32)
        nb = sb("nb", [128, 8], F32)
        gg = gate
        lnb = sb("lnb", [128, 8], F32)
        rstd = sb("rstd", [128, 8], F32)
        mst = [sb(f"mst{i}", [128, 8, 64], BF16) for i in range(2)]
        msl = [K.slot(f"Cm{i}") for i in range(2)]
        gsl = [K.slot(f"Cg{i}") for i in range(2)]
        hsl = K.slot(f"Ch")
        hsl2 = K.slot(f"Ch2")
        kvp = [psum(f"kvp{i}", [128, 8, 64], F32) for i in range(2)]
        scp = [psum(f"scp{i}", [128, 128], F32) for i in range(4)]
        yp = [psum(f"yp{i}", [128, 8, 64], F32) for i in range(2)]
        store_evs = []
        nmix = 0
        gn_def = []
        def tables(h):
            K.op('act', lambda e: e.activation(out=e1.t[:], in_=rc.t[:, 0:128], func=AF.Exp, scale=lg.t[:, h:h + 1]),
                 outs=[e1], ins=[rc, lg])
            K.op('act', lambda e: e.activation(out=e2.t[:], in_=rc.t[:, 256:384], func=AF.Exp, scale=lg.t[:, 8 + h:9 + h]),
                 outs=[e2], ins=[rc, lg])
            K.op('dve', lambda e: e.tensor_tensor(out=e1.t[:], in0=e1.t[:], in1=rc.t[:, 128:256], op=ALU.mult), outs=[e1], ins=[e1])
            K.op('dve', lambda e: e.tensor_tensor(out=e2.t[:], in0=e2.t[:], in1=rc.t[:, 384:512], op=ALU.mult), outs=[e2], ins=[e2])
            K.op('dve', lambda e: e.tensor_tensor(out=MT.t[:], in0=e1.t[:], in1=e2.t[:], op=ALU.add), outs=[MT], ins=[e1, e2])
            K.op('act', lambda e: e.activation(out=G.t[:], in_=rc.t[:, 512:640], func=AF.Exp, scale=lgsel.t[:, h:h + 1]),
                 outs=[G], ins=[rc, lgsel])

        def pro_steps(h):
            vS_, Spn = vSb[h % 2], Spb[h % 2]

            def s_load():
                K.W('sp', ktm, vS_)
                hsl.dma('sp', ktm.t[:], T['RK'][:, h * 64:(h + 1) * 64].rearrange("(n p) d -> p n d", p=128))
                hev = hsl.dma('sp', vS_.t[:], T['RV'][:, h * 64:(h + 1) * 64].rearrange("(n p) d -> p n d", p=128))
                K.setw(hev, ktm, vS_)

            def s_kfb():
                for d in range(2):
                    K.op('dve', lambda e: e.tensor_scalar(out=kfb.t[:, :, d * 64:(d + 1) * 64], in0=ktm.t[:], scalar1=scK.t[:, h, d:d + 1],
                                                          scalar2=None, op0=ALU.mult), outs=[kfb], ins=[ktm, scK], partial=(d > 0))

            def s_p1(g):
                def run():
                    kp = kvp[g % 2]
                    K.R('pe', kfb, vS_)
                    K.W('pe', kp)
                    for j in range(8):
                        nn = g * 8 + j
                        ins = nc.tensor.matmul(kp.t[:, j, :], lhsT=kfb.t[:, nn, :], rhs=vS_.t[:, nn, :], start=True, stop=True)
                    ev = K.sig('pe', ins)
                    K.setw(ev, kp)
                    K.setr(ev, kfb, vS_)
                    K.op('dve', lambda e: e.tensor_copy(out=kvT.t[0:64, :, g * 8:(g + 1) * 8].rearrange("p e t -> p t e"),
                                                        in_=kp.t[0:64, :, :]), outs=[kvT], ins=[kp], partial=(g > 0))
                    b0 = kvT.t[64:128, 0, 31 - g * 8:32 - g * 8]
                    rev_out = bass.AP(kvT.t, b0.offset, [list(b0.ap[0]), [-1, 8], [32, 64]])
                    K.op('act', lambda e: e.copy(out=rev_out, in_=kp.t[64:128, :, :]), outs=[kvT], ins=[kp], partial=True)
                return run

            def s_scan():
                K.op('dve', lambda e: e.tensor_scalar(out=gm.t[:, :, 1:32], in0=gm.t[:, :, 1:32], scalar1=0.0, scalar2=gC.t[:, h:h + 1],
                                                      op0=ALU.mult, op1=ALU.add), outs=[gm], ins=[gm, gC])
                K.R('dve', gm)
                K.W('dve', kvT)
                for e8 in range(8):
                    ins = nc.vector.tensor_tensor_scan(out=kvT.t[:, e8 * 8:(e8 + 1) * 8, :].rearrange("p e t -> p (e t)"),
                                                       data0=gm.t[:].rearrange("p e t -> p (e t)"),
                                                       data1=kvT.t[:, e8 * 8:(e8 + 1) * 8, :].rearrange("p e t -> p (e t)"), initial=0.0,
                                                       op0=ALU.mult, op1=ALU.add)
                ev = K.sig('dve', ins)
                K.setw(ev, kvT)
                K.setr(ev, gm)

            def s_sperm():
                K.op('dve', lambda e: e.tensor_copy(out=Spn.t[0:64, 1:32, :], in_=kvT.t[0:64, :, 0:31].rearrange("p e t -> p t e")),
                     outs=[Spn], ins=[kvT])
                b1 = kvT.t[64:128, 0, 30:31]
                rev_in = bass.AP(kvT.t, b1.offset, [list(b1.ap[0]), [-1, 31], [32, 64]])
                K.op('act', lambda e: e.copy(out=Spn.t[64:128, 0:31, :], in_=rev_in), outs=[Spn], ins=[kvT], partial=True)
            return [s_load, s_kfb, s_p1(0), s_p1(1), s_p1(2), s_p1(3), s_scan, s_sperm]

        SCHED = {0: [0], 2: [1], 4: [2], 7: [3], 10: [4], 13: [5], 17: [6], 21: [7]}
        for st_ in pro_steps(0):
            st_()
        for h in range(8):
            vS, Sp_ = vSb[h % 2], Spb[h % 2]
            nxt = pro_steps(h + 1) if h + 1 < 8 else None
            tables(h)
            K.W('sp', qT2, kTr)
            hsl2.dma('sp', qT2.t[0:64, :], T['RQT'][h])
            hsl2.dma('sp', qT2.t[64:128, :], T['RQT'][h])
            hev = hsl2.dma('sp', kTr.t[:, :], T['RKT'][h])
            K.setw(hev, qT2, kTr)
            K.op('dve', lambda e: e.tensor_tensor(out=qfb.t[:].rearrange("p (n c) -> p n c", n=32),
                                                   in0=qT2.t[:].rearrange("p (n c) -> p n c", n=32),
                                                   in1=G.t[:].unsqueeze(1).to_broadcast([128, 32, 128]), op=ALU.mult),
                 outs=[qfb], ins=[qT2, G])
            DEPTH = 3

            def scores(nn):
                cols = slice(nn * 128, (nn + 1) * 128)
                sp = scp[nn % 4]
                K.R('pe', kTr, qT2)
                K.W('pe', sp)
                ev = K.sig('pe', nc.tensor.matmul(sp.t[:], lhsT=kTr.t[0:64, cols], rhs=qT2.t[0:64, cols], start=True, stop=True))
                K.setw(ev, sp)
                K.setr(ev, kTr, qT2)
                ab = A_sb[nn % 4]
                K.op('dve', lambda e: e.tensor_tensor(out=ab.t[:], in0=sp.t[:], in1=MT.t[:], op=ALU.mult), outs=[ab], ins=[sp, MT])

            def ymm(pn, h=h):
                nonlocal nmix
                g, pj = divmod(pn, 8)
                ypb = yp[g % 2]
                pab = A_sb[pn % 4]
                pc = slice(pn * 128, (pn + 1) * 128)
                K.R('pe', pab, vS, qfb, Sp_)
                if pj == 0:
                    K.W('pe', ypb)
                mms = [(pab.t[:], vS.t[:, pn, :]), (qfb.t[:, pc], Sp_.t[:, pn, :])]
                for mi, (lt, rh) in enumerate(mms):
                    ins = nc.tensor.matmul(ypb.t[:, pj, :], lhsT=lt, rhs=rh, start=(mi == 0), stop=(mi == len(mms) - 1))
                ev = K.sig('pe', ins)
                K.setr(ev, pab, vS, qfb, Sp_)
                if pj != 7:
                    return
                K.setw(ev, ypb)
                st_ = gn_tail(g, ypb, h)
                next(st_)
                gn_def.append((pn + 2, st_, h))

            def gn_tail(g, ypb, h):
                nonlocal nmix
                gb, ggb = gate[(h * 4 + g) % 2], gg[(h * 4 + g) % 2]
                K.op('act', lambda e: e.copy(out=ysb.t[:], in_=ypb.t[:]), outs=[ysb], ins=[ypb])
                K.op('act', lambda e: e.activation(out=ysq.t[:], in_=ysb.t[:], func=AF.Square), outs=[ysq], ins=[ysb])
                K.op('dve', lambda e: e.tensor_reduce(out=s1.t[:], in_=ysb.t[:], axis=AX.X, op=ALU.add), outs=[s1], ins=[ysb])
                K.op('dve', lambda e: e.tensor_reduce(out=s2.t[:], in_=ysq.t[:], axis=AX.X, op=ALU.add), outs=[s2], ins=[ysq])
                K.op('dve', lambda e: e.tensor_scalar(out=s1.t[:], in0=s1.t[:], scalar1=1.0 / 64, scalar2=None, op0=ALU.mult), outs=[s1], ins=[s1])
                K.op('dve', lambda e: e.tensor_tensor(out=var.t[:], in0=s1.t[:], in1=s1.t[:], op=ALU.mult), outs=[var], ins=[s1])
                K.op('dve', lambda e: e.scalar_tensor_tensor(out=var.t[:], in0=s2.t[:], scalar=1.0 / 64, in1=var.t[:], op0=ALU.mult,
                                                             op1=ALU.subtract), outs=[var], ins=[s2, var])
                K.op('dve', lambda e: e.tensor_scalar(out=lnb.t[:, 0:8], in0=var.t[:, 0:8], scalar1=1.0, scalar2=EPS,
                                                      op0=ALU.mult, op1=ALU.add), outs=[lnb], ins=[var])
                yield 2
                K.op('act', lambda e: e.activation(out=lnb.t[:, 0:8], in_=lnb.t[:, 0:8], func=AF.Ln), outs=[lnb], ins=[lnb])
                K.op('act', lambda e: e.activation(out=rstd.t[:, 0:8], in_=lnb.t[:, 0:8], func=AF.Exp, scale=-0.5), outs=[rstd], ins=[lnb])
                yield 2
                K.op('dve', lambda e: e.scalar_tensor_tensor(out=nb.t[:], in0=s1.t[:], scalar=-1.0, in1=rstd.t[:], op0=ALU.mult,
                                                             op1=ALU.mult), outs=[nb], ins=[s1, rstd])
                for jj in range(8):
                    if jj % 2 == 0:
                        K.op('act', lambda e: e.activation(out=ysq.t[:, jj, :], in_=ysb.t[:, jj, :], func=AF.Identity,
                                                           scale=rstd.t[:, jj:jj + 1], bias=nb.t[:, jj:jj + 1]),
                             outs=[ysq], ins=[ysb, nb, rstd], partial=(jj > 0))
                    else:
                        K.op('dve', lambda e: e.tensor_scalar(out=ysq.t[:, jj, :], in0=ysb.t[:, jj, :], scalar1=s1.t[:, jj:jj + 1],
                                                              scalar2=rstd.t[:, jj:jj + 1], op0=ALU.subtract, op1=ALU.mult),
                             outs=[ysq], ins=[ysb, s1, rstd], partial=True)
                yield 3
                mb = mst[nmix % 2]
                ms_ = msl[nmix % 2]
                nmix += 1
                K.op('dve', lambda e: e.tensor_tensor(out=mb.t[:], in0=ysq.t[:], in1=ggb.t[:], op=ALU.mult), outs=[mb], ins=[ysq, ggb])
                K.R('pool', mb)
                dst = T['MIX'][g * 1024:(g + 1) * 1024, 512 + h * 64:512 + (h + 1) * 64].rearrange("(n p) d -> p n d", p=128)
                ev = ms_.dma('pool', dst, mb.t[:])
                K.setr(ev, mb)
                store_evs.append(ev)
                yield

            for nn in range(32 + DEPTH):
                if nxt is not None and nn in SCHED:
                    for si in SCHED[nn]:
                        nxt[si]()
                if nn < 32:
                    if nn % 8 == 2:
                        g = nn // 8
                        gb, ggb = gate[(h * 4 + g) % 2], gg[(h * 4 + g) % 2]
                        gs = gsl[(h * 4 + g) % 2]
                        K.W('sp', gb)
                        gev = gs.dma('sp', gb.t[:], T['GATE'][g * 1024:(g + 1) * 1024, h * 64:(h + 1) * 64].rearrange("(n p) d -> p n d", p=128))
                        K.setw(gev, gb)
                        K.op('pool', lambda e: e.tensor_tensor(out=gb.t[:], in0=gb.t[:],
                                                               in1=gng.t[:, h * 64:(h + 1) * 64].unsqueeze(1).to_broadcast([128, 8, 64]),
                                                               op=ALU.mult), outs=[gb], ins=[gb, gng])
                    scores(nn)
                if nn - DEPTH >= 0:
                    ymm(nn - DEPTH)
                    now = nn - DEPTH
                    for _ in range(len(gn_def)):
                        due, gen_, hh = gn_def[0]
                        if (hh == h and due <= now) or (hh != h and nn >= 2):
                            gn_def.pop(0)
                            try:
                                dly = next(gen_)
                                gn_def.append((now + (dly or 1), gen_, h))
                            except StopIteration:
                                pass
                        else:
                            break
        while gn_def:
            gen_ = gn_def.pop(0)[1]
            for _ in gen_:
                pass
        K.barrier(store_evs)


def phase_D(K, l, cast_ev):
    nc, T = K.nc, K.T
    with ExitStack() as es:
        def sb(n, shp, dt):
            return Buf(es.enter_context(nc.sbuf_tensor(f"D{l}_{n}", shp, dt)))

        def psum(n, shp, dt):
            return Buf(es.enter_context(nc.psum_tensor(f"D{l}_{n}", shp, dt)))
        mixb = [sb(f"mix{i}", [128, 4, 1024], BF16) for i in range(2)]
        mT = [sb(f"mT{i}", [128, 8, 512], BF16) for i in range(2)]
        Wb = [sb(f"W{i}", [128, 8, 512], BF16) for i in range(2)]
        Wsl = [K.slot(f"DW{i}") for i in range(2)]
        msl = [K.slot(f"Dm{i}") for i in range(2)]
        tp = [psum(f"tp{i}", [128, 8, 128], BF16) for i in range(2)]
        mm = [psum(f"mm{i}", [128, 512], F32) for i in range(4)]
        wv = T['wb_out'][l].rearrange("(k p) c -> p k c", p=128)
        chunks = [(tg, ct) for tg in range(8) for ct in range(2)]
        nmm = 0

        def issue(i):
            tg, ct = chunks[i]
            b = Wb[i % 2]
            K.W('sp', b)
            if i == 0:
                K.wait('sp', cast_ev)
            ev = Wsl[i % 2].dma('sp', b.t[:], wv[:, :, ct * 512:(ct + 1) * 512])
            K.setw(ev, b)

        def load_mix(tg):
            b = mixb[tg % 2]
            K.W('sp', b)
            ev = msl[tg % 2].dma('sp', b.t[:], T['MIX'][tg * 512:(tg + 1) * 512, :].rearrange("(q p) d -> p q d", p=128))
            K.setw(ev, b)

        def transpose_group(tg):
            b = mixb[tg % 2]
            mt = mT[tg % 2]
            for tt in range(4):
                tpb = tp[tt % 2]
                K.R('pe', b)
                K.W('pe', tpb)
                for k in range(8):
                    ins = nc.tensor.transpose(out=tpb.t[:, k, :], in_=b.t[:, tt, k * 128:(k + 1) * 128], identity=K.ident[:])
                ev = K.sig('pe', ins)
                K.setw(ev, tpb)
                K.setr(ev, b)
                K.op('act', lambda e: e.copy(out=mt.t[:, :, tt * 128:(tt + 1) * 128], in_=tpb.t[:]), outs=[mt], ins=[tpb],
                     partial=(tt > 0))
        load_mix(0)
        load_mix(1)
        issue(0)
        transpose_group(0)
        for i, (tg, ct) in enumerate(chunks):
            if i + 1 < len(chunks):
                issue(i + 1)
            if ct == 1 and tg + 1 < 8:
                transpose_group(tg + 1)
                if tg + 2 < 8:
                    load_mix(tg + 2)
            wb = Wb[i % 2]
            mt = mT[tg % 2]
            for tt in range(4):
                t = tg * 4 + tt
                m = mm[nmm % 4]
                nmm += 1
                K.R('pe', wb, mt)
                K.W('pe', m)
                for k in range(8):
                    ins = nc.tensor.matmul(m.t[:], lhsT=mt.t[:, k, tt * 128:(tt + 1) * 128], rhs=wb.t[:, k, :], start=(k == 0), stop=(k == 7))
                ev = K.sig('pe', ins)
                K.setw(ev, m)
                K.setr(ev, wb, mt)
                xt = K.x[:, t, ct * 512:(ct + 1) * 512]
                K.R('dve', m)
                ev = K.sig('dve', nc.vector.tensor_tensor(out=xt, in0=m.t[:], in1=xt, op=ALU.add))
                K.setr(ev, m)
        K.barrier()


def phase_E(K, l, cast1, cast2):
    nc, T = K.nc, K.T
    with ExitStack() as es:
        def sb(n, shp, dt):
            return Buf(es.enter_context(nc.sbuf_tensor(f"E{l}_{n}", shp, dt)))

        def psum(n, shp, dt):
            return Buf(es.enter_context(nc.psum_tensor(f"E{l}_{n}", shp, dt)))
        g2b = sb("g2b", [128, 1024], F32)
        psl = K.slot(f"Epar")
        pev = psl.dma('sp', g2b.t[:], pvv(K, l, "n2").partition_broadcast(128))
        K.setw(pev, g2b)
        hbf = [sb(f"hbf{i}", [128, 1024], BF16) for i in range(2)]
        hT = [sb(f"hT{i}", [128, 8, 512], BF16) for i in range(2)]
        uT = sb("uT", [128, 32, 512], BF16)
        Wb = [sb(f"W{i}", [128, 8, 512], BF16) for i in range(2)]
        Wsl = [K.slot(f"EW{i}") for i in range(2)]
        r32 = [sb(f"r32{i}", [128, 512], F32) for i in range(2)]
        st = [sb(f"st{i}", [128, 12], F32) for i in range(2)]
        mv = [sb(f"mv{i}", [128, 2], F32) for i in range(2)]
        ms = [sb(f"ms{i}", [128, 1], F32) for i in range(2)]
        lnb = [sb(f"lnb{i}", [128, 8], F32) for i in range(2)]
        rs = [sb(f"rs{i}", [128, 8], F32) for i in range(2)]
        tp = [psum(f"tp{i}", [128, 8, 128], BF16) for i in range(2)]
        hp = [psum(f"hp{i}", [128, 512], F32) for i in range(2)]
        acc = [psum(f"acc{i}", [128, 512], F32) for i in range(4)]
        w1v = T['wb1'][l].rearrange("(k p) c -> p k c", p=128)
        w2v = T['wb2'][l].rearrange("(f p) c -> p f c", p=128)
        chunks = []
        for tg in range(8):
            chunks += [(tg, 1, fc, 0) for fc in range(8)]
            chunks += [(tg, 2, fg, ct) for ct in range(2) for fg in range(8)]

        def issue(i):
            tg, kind, a, ct = chunks[i]
            b = Wb[i % 2]
            K.W('sp', b)
            if i == 0:
                K.wait('sp', cast1)
                K.wait('sp', cast2)
            if kind == 1:
                ev = Wsl[i % 2].dma('sp', b.t[:], w1v[:, :, a * 512:(a + 1) * 512])
            else:
                ev = Wsl[i % 2].dma('sp', b.t[:, 0:4, :], w2v[:, a * 4:(a + 1) * 4, ct * 512:(ct + 1) * 512])
            K.setw(ev, b)

        chain_a, chain_b, tr = make_norm(K, g2b, hbf, hT, tp, st, mv, ms, lnb, rs)
        issue(0)
        for t0_ in (0, 2):
            chain_a(0, t0_)
            chain_a(0, t0_ + 1)
            chain_b(0, t0_)
            chain_b(0, t0_ + 1)
            tr(0, t0_)
            tr(0, t0_ + 1)
        nh = 0
        for i, (tg, kind, a, ct) in enumerate(chunks):
            if i + 1 < len(chunks):
                issue(i + 1)
            gi = i % 24
            if tg + 1 < 8 and gi >= 8 and (gi - 8) % 3 == 0 and (gi - 8) // 3 < 4:
                chain_a(tg + 1, (gi - 8) // 3)
            if tg + 1 < 8 and gi >= 9 and (gi - 9) % 3 == 0 and (gi - 9) // 3 < 4:
                chain_b(tg + 1, (gi - 9) // 3)
            if tg + 1 < 8 and gi >= 11 and (gi - 11) % 3 == 0 and (gi - 11) // 3 < 4:
                tr(tg + 1, (gi - 11) // 3)
            wb = Wb[i % 2]
            hb = hT[tg % 2]
            if kind == 1:
                for f4 in range(4):
                    f = a * 4 + f4
                    p = hp[nh % 2]
                    r = r32[nh % 2]
                    nh += 1
                    K.R('pe', wb, hb)
                    K.W('pe', p)
                    for k in range(8):
                        ins = nc.tensor.matmul(p.t[:], lhsT=wb.t[:, k, f4 * 128:(f4 + 1) * 128], rhs=hb.t[:, k, :], start=(k == 0), stop=(k == 7))
                    ev = K.sig('pe', ins)
                    K.setw(ev, p)
                    K.setr(ev, wb, hb)
                    K.op('act', lambda e: e.activation(out=r.t[:], in_=p.t[:], func=AF.Relu), outs=[r], ins=[p])
                    K.op('dve', lambda e: e.tensor_tensor(out=uT.t[:, f, :], in0=r.t[:], in1=r.t[:], op=ALU.mult), outs=[uT], ins=[r],
                         partial=(f > 0))
            else:
                K.R('pe', wb, uT)
                for fi in range(4):
                    f = a * 4 + fi
                    for tt in range(4):
                        if f == 0:
                            K.W('pe', acc[tt])
                        ins = nc.tensor.matmul(acc[tt].t[:], lhsT=uT.t[:, f, tt * 128:(tt + 1) * 128], rhs=wb.t[:, fi, :],
                                               start=(f == 0), stop=(f == 31))
                        if f == 31:
                            ev = K.sig('pe', ins)
                            K.setw(ev, acc[tt])
                if a != 7:
                    ev = K.sig('pe', ins)
                K.setr(ev, wb, uT)
                if a == 7:
                    for tt in range(4):
                        t = tg * 4 + tt
                        xt = K.x[:, t, ct * 512:(ct + 1) * 512]
                        K.R('dve', acc[tt])
                        ev2 = K.sig('dve', nc.vector.tensor_tensor(out=xt, in0=acc[tt].t[:], in1=xt, op=ALU.add))
                        K.setr(ev2, acc[tt])
        K.barrier()


def _consts():
    bf = ml_dtypes.bfloat16
    pos = np.arange(S)
    ih = (pos // 128).astype(np.float32)
    il = (pos % 128).astype(np.float32)
    one = np.ones(S, np.float32)
    zero = np.zeros(S, np.float32)
    qaug = np.stack([np.stack([ih, il, one, one, zero, zero, zero, zero]),
                     np.stack([zero, zero, zero, zero, ih, il, one, one])]).astype(bf)
    kaug = []
    for s in SLOPES:
        a = np.stack([-128 * s * one, -s * one, 128 * s * ih, s * il])
        kaug.append(np.concatenate([a, -a], axis=0))
    kaug = np.stack(kaug).astype(bf)
    j = np.arange(128)[:, None].astype(np.float32)
    c = np.arange(128)[None, :].astype(np.float32)
    bd = np.stack([-s * np.abs(c - j) for s in SLOPES], axis=1).astype(bf)
    rc = np.zeros((128, 5 * 128 + 2), np.float32)
    rc[:, 0:128] = np.maximum(c - j, 0)
    rc[:, 128:256] = (c >= j)
    rc[:, 256:384] = np.maximum(j - c, 0)
    rc[:, 384:512] = (j > c)
    rc[0:64, 512:640] = c + 1.0
    rc[64:128, 512:640] = 128.0 - c
    rc[:, 640] = 127.0 - j[:, 0]
    rc[:, 641] = j[:, 0]
    return dict(qaug=qaug, kaug=kaug, bd=bd, rc=rc)


_PROGS = {}


def _get_prog(L, lam_layers, debug=False):
    key = (L, tuple(lam_layers), debug)
    if key not in _PROGS:
        _PROGS[key] = build_program(L, lam_layers, debug)
    return _PROGS[key]


def _pack_pv(inp, layers):
    rows = []
    for l in layers:
        rows.append(np.concatenate([
            inp['norm1_g'][l], inp['norm2_g'][l], inp['q_norm_g'][l], inp['k_norm_g'][l],
            inp['lambda_q1'][l].reshape(-1), inp['lambda_k1'][l].reshape(-1), inp['lambda_q2'][l].reshape(-1),
            inp['lambda_k2'][l].reshape(-1), inp['diff_out_g'][l], inp['ret_decay_fwd'][l], inp['ret_decay_bwd'][l],
            inp['ret_gn_g'][l]]).astype(np.float32))
    return np.ascontiguousarray(np.stack(rows))


FUSED = True


def kernel(**inp):
    inp = {k: np.asarray(v) for k, v in inp.items()}
    x = np.ascontiguousarray(inp['x'], dtype=np.float32)
    B = x.shape[0]
    cst = _consts()
    groups = [list(range(4))] if FUSED else [[l] for l in range(4)]
    cur = [x[b] for b in range(B)]
    for layers in groups:
        nc = _get_prog(len(layers), layers)
        shared = dict(w_in=np.ascontiguousarray(inp['w_in'][layers]), w_out=np.ascontiguousarray(inp['w_out'][layers]),
                      w1=np.ascontiguousarray(inp['w_mlp1'][layers]), w2=np.ascontiguousarray(inp['w_mlp2'][layers]),
                      pv=_pack_pv(inp, layers), **cst)
        in_maps = [dict(x=np.ascontiguousarray(cur[b]), **shared) for b in range(B)]
        res = run_bass_kernel_spmd(nc, in_maps, core_ids=list(range(B)))
        cur = [np.asarray(res.results[b]['y'], dtype=np.float32) for b in range(B)]
    return np.stack(cur).astype(np.float32)
```

```python
import math
from contextlib import ExitStack
import numpy as np
import ml_dtypes
import concourse.bass as bass
import concourse.mybir as mybir
from concourse.bass_utils import run_bass_kernel_spmd

F32 = mybir.dt.float32
BF16 = mybir.dt.bfloat16
ALU = mybir.AluOpType
AF = mybir.ActivationFunctionType
AX = mybir.AxisListType
S = 4096
D = 1024
NT = 32
EPS = 1e-6
OFF = {}
_o = 0
for _nm, _n in [("n1", 1024), ("n2", 1024), ("qg", 64), ("kg", 64), ("lq1", 256), ("lk1", 256), ("lq2", 256),
                ("lk2", 256), ("dog", 512), ("rdf", 8), ("rdb", 8), ("gng", 512)]:
    OFF[_nm] = (_o, _n)
    _o += _n
NPV = _o
SLOPES = [2.0 ** (-8.0 * (h + 1) / 4) for h in range(4)]


class Buf:
    def __init__(self, t):
        self.t = t
        self.ready = []
        self.readers = []


class Slot:
    def __init__(self, K, name):
        self.K, self.name, self.sem, self.v = K, name, None, 0

    def dma(self, q, out, in_):
        K = self.K
        if self.sem is None or self.v + 16 > K.LIM:
            self.sem = K.newsem("d_" + self.name)
            self.v = 0
        ins = K.E[q].dma_start(out=out, in_=in_)
        ins.then_inc(self.sem, 16)
        self.v += 16
        return (self.sem, self.v)


class KB:
    LIM = 30000

    def __init__(self, nc, es):
        self.nc, self.es = nc, es
        self.E = dict(pe=nc.tensor, act=nc.scalar, dve=nc.vector, pool=nc.gpsimd, sp=nc.sync)
        self.cur, self.last, self.waited, self.nsem = {}, {}, {}, 0
        self.slots = {}

    def slot(self, name):
        if name not in self.slots:
            self.slots[name] = Slot(self, name)
        return self.slots[name]

    def newsem(self, name):
        self.nsem += 1
        return self.es.enter_context(self.nc.semaphore(f"{name}_{self.nsem}"))

    def sig(self, e, ins):
        c = self.cur.get(e)
        if c is None or c[1] + 1 > self.LIM:
            c = [self.newsem("c_" + e), 0]
            self.cur[e] = c
        ins.then_inc(c[0], 1)
        c[1] += 1
        ev = (c[0], c[1])
        self.last[e] = ev
        return ev

    def wait(self, e, ev):
        if ev is None:
            return
        sem, v = ev
        key = (e, sem.num)
        if self.waited.get(key, 0) >= v:
            return
        self.E[e].wait_ge(sem, v)
        self.waited[key] = v

    def W(self, e, *bufs):
        for b in bufs:
            for ev in b.readers + b.ready:
                self.wait(e, ev)

    def R(self, e, *bufs):
        for b in bufs:
            for ev in b.ready:
                self.wait(e, ev)

    def setw(self, ev, *bufs):
        for b in bufs:
            b.ready = [ev]
            b.readers = []

    def addw(self, ev, *bufs):
        for b in bufs:
            b.ready.append(ev)

    def setr(self, ev, *bufs):
        for b in bufs:
            b.readers.append(ev)

    def op(self, e, fn, outs=(), ins=(), partial=False):
        for b in ins:
            self.R(e, b)
        for b in outs:
            if partial:
                for ev in b.readers:
                    self.wait(e, ev)
            else:
                self.W(e, b)
        ev = self.sig(e, fn(self.E[e]))
        for b in outs:
            if partial:
                self.addw(ev, b)
            else:
                self.setw(ev, b)
        for b in ins:
            self.setr(ev, b)
        return ev

    def barrier(self, store_evs=()):
        for ev in store_evs:
            self.wait('pool', ev)
        self.sig('pool', self.E['pool'].memset(self.mark[:, 0:1], 0.0))
        for e in ('pe', 'act', 'dve', 'pool', 'sp'):
            for f in ('pe', 'act', 'dve', 'pool'):
                self.wait(e, self.last.get(f))


def lambda_init(layer):
    return 0.8 - 0.6 * math.exp(-0.3 * layer)


def rstd_chain(K, src, dst, lnb, n_inv, cols):
    K.op('dve', lambda e: e.tensor_scalar(out=lnb.t[:, 0:cols], in0=src.t[:, 0:cols], scalar1=n_inv, scalar2=EPS,
                                          op0=ALU.mult, op1=ALU.add), outs=[lnb], ins=[src])
    K.op('act', lambda e: e.activation(out=lnb.t[:, 0:cols], in_=lnb.t[:, 0:cols], func=AF.Ln), outs=[lnb], ins=[lnb])
    K.op('act', lambda e: e.activation(out=dst.t[:, 0:cols], in_=lnb.t[:, 0:cols], func=AF.Exp, scale=-0.5),
         outs=[dst], ins=[lnb])


def make_norm(K, gb, hbf, hT, tp, st, mv, ms, lnb, rs):
    nc = K.nc

    def chain_a(tg, tt):
        t = tg * 4 + tt
        i2 = t % 2
        xt = K.x[:, t, :]
        K.op('dve', lambda e: e.bn_stats(out=st[i2].t[:, 0:6], in_=xt[:, 0:512]), outs=[st[i2]])
        K.op('dve', lambda e: e.bn_stats(out=st[i2].t[:, 6:12], in_=xt[:, 512:1024]), outs=[st[i2]], partial=True)
        K.op('dve', lambda e: e.bn_aggr(out=mv[i2].t[:, 0:2], in_=st[i2].t[:, 0:12]), outs=[mv[i2]], ins=[st[i2]])
        K.op('dve', lambda e: e.scalar_tensor_tensor(out=ms[i2].t[:, 0:1], in0=mv[i2].t[:, 0:1], scalar=mv[i2].t[:, 0:1],
                                                     in1=mv[i2].t[:, 1:2], op0=ALU.mult, op1=ALU.add),
             outs=[ms[i2]], ins=[mv[i2]])
        K.op('dve', lambda e: e.tensor_scalar(out=lnb[i2].t[:, 0:1], in0=ms[i2].t[:, 0:1], scalar1=1.0, scalar2=EPS,
                                              op0=ALU.mult, op1=ALU.add), outs=[lnb[i2]], ins=[ms[i2]])

    def chain_b(tg, tt):
        t = tg * 4 + tt
        i2 = t % 2
        xt = K.x[:, t, :]
        K.op('act', lambda e: e.activation(out=lnb[i2].t[:, 0:1], in_=lnb[i2].t[:, 0:1], func=AF.Ln), outs=[lnb[i2]], ins=[lnb[i2]])
        K.op('act', lambda e: e.activation(out=rs[i2].t[:, 0:1], in_=lnb[i2].t[:, 0:1], func=AF.Exp, scale=-0.5),
             outs=[rs[i2]], ins=[lnb[i2]])
        K.op('dve', lambda e: e.scalar_tensor_tensor(out=hbf[i2].t[:], in0=xt, scalar=rs[i2].t[:, 0:1], in1=gb.t[:],
                                                     op0=ALU.mult, op1=ALU.mult), outs=[hbf[i2]], ins=[rs[i2], gb])

    def tr(tg, tt):
        t = tg * 4 + tt
        i2 = t % 2
        hb = hT[tg % 2]
        tpb = tp[i2]
        K.R('pe', hbf[i2])
        K.W('pe', tpb)
        for k in range(8):
            ins = nc.tensor.transpose(out=tpb.t[:, k, :], in_=hbf[i2].t[:, k * 128:(k + 1) * 128], identity=K.ident[:])
        ev = K.sig('pe', ins)
        K.setw(ev, tpb)
        K.setr(ev, hbf[i2])
        K.op('act', lambda e: e.copy(out=hb.t[:, :, tt * 128:(tt + 1) * 128], in_=tpb.t[:]), outs=[hb], ins=[tpb],
             partial=(tt > 0))
    return chain_a, chain_b, tr


def build_program(L, lam_layers, debug=False, phases="ABCDE"):
    nc = bass.Bass("TRN2", target_bir_lowering=False)
    dk = "ExternalOutput" if debug else "Internal"
    T = {}
    T['x'] = nc.dram_tensor("x", [S, D], F32, kind="ExternalInput").ap()
    T['y'] = nc.dram_tensor("y", [S, D], F32, kind="ExternalOutput").ap()
    T['w_in'] = nc.dram_tensor("w_in", [L, D, 3584], F32, kind="ExternalInput").ap()
    T['w_out'] = nc.dram_tensor("w_out", [L, D, D], F32, kind="ExternalInput").ap()
    T['w1'] = nc.dram_tensor("w1", [L, D, 4096], F32, kind="ExternalInput").ap()
    T['w2'] = nc.dram_tensor("w2", [L, 4096, D], F32, kind="ExternalInput").ap()
    T['pv'] = nc.dram_tensor("pv", [L, NPV], F32, kind="ExternalInput").ap()
    T['qaug'] = nc.dram_tensor("qaug", [2, 8, S], BF16, kind="ExternalInput").ap()
    T['kaug'] = nc.dram_tensor("kaug", [4, 8, S], BF16, kind="ExternalInput").ap()
    T['bd'] = nc.dram_tensor("bd", [128, 4, 128], BF16, kind="ExternalInput").ap()
    T['rc'] = nc.dram_tensor("rc", [128, 5 * 128 + 2], F32, kind="ExternalInput").ap()
    T['wb_in'] = nc.dram_tensor("wb_in", [L, D, 3584], BF16, kind="Internal").ap()
    T['wb_out'] = nc.dram_tensor("wb_out", [L, D, D], BF16, kind="Internal").ap()
    T['wb1'] = nc.dram_tensor("wb1", [L, D, 4096], BF16, kind="Internal").ap()
    T['wb2'] = nc.dram_tensor("wb2", [L, 4096, D], BF16, kind="Internal").ap()
    T['QT'] = nc.dram_tensor("QT", [4, 2, 64, S], BF16, kind=dk).ap()
    T['KT'] = nc.dram_tensor("KT", [4, 2, 64, S], BF16, kind=dk).ap()
    T['V'] = nc.dram_tensor("V", [S, 512], BF16, kind=dk).ap()
    T['RQT'] = nc.dram_tensor("RQT", [8, 64, S], BF16, kind=dk).ap()
    T['RKT'] = nc.dram_tensor("RKT", [8, 64, S], BF16, kind=dk).ap()
    T['RK'] = nc.dram_tensor("RK", [S, 512], BF16, kind=dk).ap()
    T['RV'] = nc.dram_tensor("RV", [S, 512], BF16, kind=dk).ap()
    T['GATE'] = nc.dram_tensor("GATE", [S, 512], F32, kind=dk).ap()
    T['MIX'] = nc.dram_tensor("MIX", [S, D], BF16, kind=dk).ap()

    with ExitStack() as es:
        K = KB(nc, es)
        K.T = T
        K.mark = es.enter_context(nc.sbuf_tensor("mark", [128, 4], F32))
        xs = es.enter_context(nc.sbuf_tensor("xres", [128, NT, D], F32))
        K.x = xs
        ident = es.enter_context(nc.sbuf_tensor("ident", [128, 128], BF16))
        K.ident = ident
        cast_ev = []
        for l in range(L):
            d = {}
            for nm, wnm, rows in (("in", "w_in", D), ("out", "w_out", D), ("1", "w1", D), ("2", "w2", 4096)):
                sl = Slot(K, f"cast{l}{nm}")
                ev = None
                for r in range(rows // 128 if "Z" not in phases else 0):
                    ev = sl.dma('pool', T['wb' + ('_' + nm if nm in ('in', 'out') else nm)][l, r * 128:(r + 1) * 128, :],
                                T[wnm][l, r * 128:(r + 1) * 128, :])
                d[nm] = ev
            cast_ev.append(d if "Z" not in phases else dict.fromkeys(['in','out','1','2']))
        K.wait('pool', K.sig('pool', nc.gpsimd.memset(ident[:], 0.0)))
        K.ident_ev = K.sig('pool', nc.gpsimd.affine_select(out=ident[:], in_=ident[:], pattern=[[-1, 128]],
                                                           compare_op=ALU.not_equal, fill=1.0, base=0,
                                                           channel_multiplier=1))
        xsl = Slot(K, "xld")
        xv = T['x'].rearrange("(t p) d -> p t d", p=128)
        for i in range(8):
            xev = xsl.dma('sp', xs[:, i * 4:(i + 1) * 4, :], xv[:, i * 4:(i + 1) * 4, :])
        for e in ('pe', 'act', 'dve', 'pool'):
            K.wait(e, xev)
            K.wait(e, K.ident_ev)
        for l in range(L):
            li = lam_layers[l]
            if "A" in phases:
                phase_A(K, l, cast_ev[l]['in'])
            if "B" in phases:
                phase_B(K, l, li)
            if "C" in phases:
                phase_C(K, l)
            if "D" in phases:
                phase_D(K, l, cast_ev[l]['out'])
            if "E" in phases:
                phase_E(K, l, cast_ev[l]['1'], cast_ev[l]['2'])
        for d in cast_ev:
            for ev in d.values():
                K.wait('sp', ev)
        osl = Slot(K, "xst")
        yv = T['y'].rearrange("(t p) d -> p t d", p=128)
        for i in range(8):
            oev = osl.dma('sp', yv[:, i * 4:(i + 1) * 4, :], xs[:, i * 4:(i + 1) * 4, :])
        nc.sync.wait_ge(oev[0], oev[1])
    return nc


def pvv(K, l, nm):
    o, n = OFF[nm]
    return K.T['pv'][l, o:o + n]


def phase_A(K, l, cast_ev):
    nc, T = K.nc, K.T
    with ExitStack() as es:
        def sb(n, shp, dt):
            return Buf(es.enter_context(nc.sbuf_tensor(f"A{l}_{n}", shp, dt)))

        def psum(n, shp, dt):
            return Buf(es.enter_context(nc.psum_tensor(f"A{l}_{n}", shp, dt)))
        g1b = sb("g1b", [128, 1024], F32)
        gqk = sb("gqk", [128, 2, 64], F32)
        psl = K.slot(f"Apar")
        psl.dma('sp', g1b.t[:], pvv(K, l, "n1").partition_broadcast(128))
        psl.dma('sp', gqk.t[:, 0, :], pvv(K, l, "qg").partition_broadcast(128))
        pev = psl.dma('sp', gqk.t[:, 1, :], pvv(K, l, "kg").partition_broadcast(128))
        K.setw(pev, g1b, gqk)
        K.op('dve', lambda e: e.tensor_scalar(out=gqk.t[:, 0, :], in0=gqk.t[:, 0, :], scalar1=0.125, scalar2=None,
                                              op0=ALU.mult), outs=[gqk], ins=[gqk])
        hbf = [sb(f"hbf{i}", [128, 1024], BF16) for i in range(2)]
        hT = [sb(f"hT{i}", [128, 8, 512], BF16) for i in range(2)]
        Wb = [sb(f"W{i}", [128, 8, 512], BF16) for i in range(2)]
        Wsl = [K.slot(f"AW{i}") for i in range(2)]
        st = [sb(f"st{i}", [128, 12], F32) for i in range(2)]
        mv = [sb(f"mv{i}", [128, 2], F32) for i in range(2)]
        ms = [sb(f"ms{i}", [128, 1], F32) for i in range(2)]
        lnb = [sb(f"lnb{i}", [128, 8], F32) for i in range(2)]
        rs = [sb(f"rs{i}", [128, 8], F32) for i in range(2)]
        qsb = [sb(f"qsb{i}", [128, 512], F32) for i in range(2)]
        sqb = sb("sqb", [128, 512], F32)
        ssq = [sb(f"ssq{i}", [128, 8], F32) for i in range(2)]
        tmpb = sb("tmpb", [128, 512], F32)
        lnq = [sb(f"lnq{i}", [128, 8], F32) for i in range(2)]
        rsq = [sb(f"rsq{i}", [128, 8], F32) for i in range(2)]
        qhat = [sb(f"qhat{i}", [128, 512], BF16) for i in range(4)]
        trst = {nm: sb(f"tr_{nm}", [128, 4, 512], BF16) for nm in ("q", "k")}
        trst["rq"], trst["rk"] = trst["q"], trst["k"]
        trsl = {nm: K.slot(f"Atr{nm}") for nm in ("q", "k", "rq", "rk")}
        tmst = {nm: [sb(f"tm_{nm}{i}", [128, 512], BF16) for i in range(4 if nm == "rk" else 3)] for nm in ("v", "rk", "rv")}
        gst = [sb(f"gst{i}", [128, 512], F32) for i in range(3)]
        tmsl = {nm: [K.slot(f"Atm{nm}{i}") for i in range(4 if nm == "rk" else 3)] for nm in ("v", "rk", "rv", "g")}
        tp = [psum(f"tp{i}", [128, 8, 128], BF16) for i in range(2)]
        mm = [psum(f"mm{i}", [128, 512], F32) for i in range(3)]
        tq = [psum(f"tq{i}", [128, 4, 128], BF16) for i in range(2)]
        cnt = dict(mm=0, tq=0, q=0, tm=0)
        store_evs = []
        chunks = [(tg, ct) for tg in range(8) for ct in range(7)]
        wv = T['wb_in'][l].rearrange("(k p) c -> p k c", p=128)

        def issue(i):
            tg, ct = chunks[i]
            b = Wb[i % 2]
            K.W('sp', b)
            if i == 0:
                K.wait('sp', cast_ev)
            ev = Wsl[i % 2].dma('sp', b.t[:], wv[:, :, ct * 512:(ct + 1) * 512])
            K.setw(ev, b)

        chain_a, chain_b, tr = make_norm(K, g1b, hbf, hT, tp, st, mv, ms, lnb, rs)

        pending_pe = []
        pending_q2 = []

        def transposes_to(src, nm, tt, tg):
            def run():
                tqb = tq[cnt['tq'] % 2]
                cnt['tq'] += 1
                K.R('pe', src)
                K.W('pe', tqb)
                for j in range(4):
                    ins = nc.tensor.transpose(out=tqb.t[:, j, :], in_=src.t[:, j * 128:(j + 1) * 128], identity=K.ident[:])
                ev = K.sig('pe', ins)
                K.setw(ev, tqb)
                K.setr(ev, src)
                stg = trst[nm]
                K.op('dve', lambda e: e.tensor_copy(out=stg.t[:, :, tt * 128:(tt + 1) * 128], in_=tqb.t[:]), outs=[stg],
                     ins=[tqb], partial=(tt > 0))
                if tt == 3:
                    K.R('pool', stg)
                    tok = slice(tg * 512, (tg + 1) * 512)
                    if nm == "q":
                        dst = T['QT'].rearrange("h c d s -> (c d) h s")[:, :, tok]
                    elif nm == "k":
                        dst = T['KT'].rearrange("h c d s -> (c d) h s")[:, :, tok]
                    elif nm == "rq":
                        dst = T['RQT'].rearrange("(p two) d s -> (two d) p s", two=2)[:, :, tok]
                    else:
                        dst = T['RKT'].rearrange("(p two) d s -> (two d) p s", two=2)[:, :, tok]
                    ev2 = trsl[nm].dma('pool', dst, stg.t[:])
                    K.setr(ev2, stg)
                    store_evs.append(ev2)
            return run

        def tm_store(nm, buf, slot, dram, t):
            K.R('pool', buf)
            ev = slot.dma('pool', dram[t * 128:(t + 1) * 128, :], buf.t[:])
            K.setr(ev, buf)
            store_evs.append(ev)

        def evac(ct, tg, tt, m):
            t = tg * 4 + tt
            if ct in (0, 1):
                i2 = cnt['q'] % 2
                i4 = cnt['q'] % 4
                cnt['q'] += 1
                K.op('act', lambda e: e.copy(out=qsb[i2].t[:], in_=m.t[:]), outs=[qsb[i2]], ins=[m])
                K.op('act', lambda e: e.activation(out=sqb.t[:], in_=m.t[:], func=AF.Square), outs=[sqb], ins=[m])
                K.op('dve', lambda e: e.tensor_reduce(out=ssq[i2].t[:], in_=sqb.t[:].rearrange("p (g d) -> p g d", g=8),
                                                      axis=AX.X, op=ALU.add), outs=[ssq[i2]], ins=[sqb])
                K.op('dve', lambda e: e.tensor_scalar(out=lnq[i2].t[:, 0:8], in0=ssq[i2].t[:, 0:8], scalar1=1.0 / 64, scalar2=EPS,
                                                      op0=ALU.mult, op1=ALU.add), outs=[lnq[i2]], ins=[ssq[i2]])

                def part2(i2=i2, i4=i4, ct=ct, tt=tt, tg=tg):
                    K.op('act', lambda e: e.activation(out=lnq[i2].t[:, 0:8], in_=lnq[i2].t[:, 0:8], func=AF.Ln), outs=[lnq[i2]], ins=[lnq[i2]])
                    K.op('act', lambda e: e.activation(out=rsq[i2].t[:, 0:8], in_=lnq[i2].t[:, 0:8], func=AF.Exp, scale=-0.5),
                         outs=[rsq[i2]], ins=[lnq[i2]])
                    K.op('dve', lambda e: e.tensor_tensor(out=tmpb.t[:].rearrange("p (g d) -> p g d", g=8),
                                                          in0=qsb[i2].t[:].rearrange("p (g d) -> p g d", g=8),
                                                          in1=rsq[i2].t[:, 0:8].unsqueeze(2).to_broadcast([128, 8, 64]), op=ALU.mult),
                         outs=[tmpb], ins=[qsb[i2], rsq[i2]])
                    K.op('dve', lambda e: e.tensor_tensor(out=qhat[i4].t[:].rearrange("p (g d) -> p g d", g=8),
                                                          in0=tmpb.t[:].rearrange("p (g d) -> p g d", g=8),
                                                          in1=gqk.t[:, ct, :].unsqueeze(1).to_broadcast([128, 8, 64]), op=ALU.mult),
                         outs=[qhat[i4]], ins=[tmpb, gqk])
                    pending_pe.append(transposes_to(qhat[i4], "q" if ct == 0 else "k", tt, tg))
                pending_q2.append(part2)
            elif ct in (2, 5):
                nm = "v" if ct == 2 else "rv"
                b = tmst[nm][t % 3]
                K.op('act', lambda e: e.copy(out=b.t[:], in_=m.t[:]), outs=[b], ins=[m])
                tm_store(nm, b, tmsl[nm][t % 3], T['V'] if ct == 2 else T['RV'], t)
            elif ct == 3:
                i4 = cnt['q'] % 4
                cnt['q'] += 1
                K.op('act', lambda e: e.copy(out=qhat[i4].t[:], in_=m.t[:]), outs=[qhat[i4]], ins=[m])
                pending_pe.append(transposes_to(qhat[i4], "rq", tt, tg))
            elif ct == 4:
                b = tmst["rk"][t % 4]
                K.op('act', lambda e: e.activation(out=b.t[:], in_=m.t[:], func=AF.Copy, scale=0.125), outs=[b], ins=[m])
                tm_store("rk", b, tmsl["rk"][t % 4], T['RK'], t)
                pending_pe.append(transposes_to(b, "rk", tt, tg))
            else:
                b = gst[t % 3]
                K.op('act', lambda e: e.activation(out=b.t[:], in_=m.t[:], func=AF.Silu), outs=[b], ins=[m])
                tm_store("g", b, tmsl["g"][t % 3], T['GATE'], t)

        issue(0)
        for t0_ in (0, 2):
            chain_a(0, t0_)
            chain_a(0, t0_ + 1)
            chain_b(0, t0_)
            chain_b(0, t0_ + 1)
            tr(0, t0_)
            tr(0, t0_ + 1)
        for i, (tg, ct) in enumerate(chunks):
            if i + 1 < len(chunks):
                issue(i + 1)
            nxt = tg + 1 < 8
            if nxt and 1 <= ct <= 4:
                chain_a(tg + 1, ct - 1)
            wb = Wb[i % 2]
            hb = hT[tg % 2]
            for tt in range(4):
                if tt == 2 and nxt and 1 <= ct <= 4:
                    chain_b(tg + 1, ct - 1)
                m = mm[cnt['mm'] % 3]
                cnt['mm'] += 1
                K.R('pe', wb, hb)
                K.W('pe', m)
                for k in range(8):
                    ins = nc.tensor.matmul(m.t[:], lhsT=hb.t[:, k, tt * 128:(tt + 1) * 128], rhs=wb.t[:, k, :],
                                           start=(k == 0), stop=(k == 7))
                ev = K.sig('pe', ins)
                K.setw(ev, m)
                K.setr(ev, wb, hb)
                todo2 = list(pending_q2)
                pending_q2.clear()
                evac(ct, tg, tt, m)
                for f in todo2:
                    f()
                while len(pending_pe) > 3:
                    pending_pe.pop(0)()
            if nxt and 2 <= ct <= 5:
                tr(tg + 1, ct - 2)
        for f in pending_q2:
            f()
        while pending_pe:
            pending_pe.pop(0)()
        K.barrier(store_evs)


BAND_THR = 64.0
EXP_SHIFT = -8.0


def kept_blocks(h, qt):
    out = []
    for kb in range(32):
        lo, hi = kb * 128, kb * 128 + 127
        if lo > qt * 512 + 511:
            dmin = lo - (qt * 512 + 511)
        elif hi < qt * 512:
            dmin = qt * 512 - hi
        else:
            dmin = 0
        if SLOPES[h] * dmin < BAND_THR:
            out.append(kb)
    return out


def phase_B(K, l, li):
    nc, T = K.nc, K.T
    lam0 = lambda_init(li)
    with ExitStack() as es:
        def sb(n, shp, dt, st=None):
            return Buf((st or es).enter_context(nc.sbuf_tensor(f"B{l}_{n}", shp, dt)))

        def psum(n, shp, dt):
            return Buf(es.enter_context(nc.psum_tensor(f"B{l}_{n}", shp, dt)))
        lms = sb("lms", [128, 8], F32)
        lam = sb("lam", [128, 4], F32)
        dog = sb("dog", [128, 512], F32)
        bd = sb("bd", [128, 4, 128], BF16)
        sbias = sb("sbias", [128, 1], F32)
        K.op('pool', lambda e: e.memset(sbias.t[:], EXP_SHIFT), outs=[sbias])
        psl = K.slot("Bpar")
        with ExitStack() as es0:
            lmb = sb("lmb", [128, 4, 256], F32, es0)
            lmp = sb("lmp", [128, 2, 256], F32, es0)
            for j, nm in enumerate(("lq1", "lk1", "lq2", "lk2")):
                psl.dma('sp', lmb.t[:, j, :], pvv(K, l, nm).partition_broadcast(128))
            psl.dma('sp', dog.t[:], pvv(K, l, "dog").partition_broadcast(128))
            pev = psl.dma('sp', bd.t[:], T['bd'])
            K.setw(pev, lmb, dog, bd)
            K.op('dve', lambda e: e.tensor_tensor(out=lmp.t[:, 0, :], in0=lmb.t[:, 0, :], in1=lmb.t[:, 1, :], op=ALU.mult),
                 outs=[lmp], ins=[lmb])
            K.op('dve', lambda e: e.tensor_tensor(out=lmp.t[:, 1, :], in0=lmb.t[:, 2, :], in1=lmb.t[:, 3, :], op=ALU.mult),
                 outs=[lmp], ins=[lmb], partial=True)
            K.op('dve', lambda e: e.tensor_reduce(out=lms.t[:, 0:8], in_=lmp.t[:].rearrange("p a (h d) -> p (a h) d", h=4),
                                                  axis=AX.X, op=ALU.add), outs=[lms], ins=[lmp])
            K.op('act', lambda e: e.activation(out=lms.t[:, 0:8], in_=lms.t[:, 0:8], func=AF.Exp), outs=[lms], ins=[lms])
            K.op('dve', lambda e: e.tensor_tensor(out=lam.t[:, 0:4], in0=lms.t[:, 0:4], in1=lms.t[:, 4:8], op=ALU.subtract),
                 outs=[lam], ins=[lms])
            K.op('dve', lambda e: e.tensor_scalar(out=lam.t[:, 0:4], in0=lam.t[:, 0:4], scalar1=lam0, scalar2=None, op0=ALU.add),
                 outs=[lam], ins=[lam])
            K.op('dve', lambda e: e.tensor_scalar(out=dog.t[:], in0=dog.t[:], scalar1=1.0 - lam0, scalar2=None, op0=ALU.mult),
                 outs=[dog], ins=[dog])
            for e_ in ('pe', 'act', 'dve', 'pool', 'sp'):
                K.R(e_, lam, dog, bd)
        kT = [sb(f"kT{i}", [72, 2, S], BF16) for i in range(2)]
        vS = [sb(f"vS{i}", [128, 32, 132], BF16) for i in range(2)]
        qS = [sb(f"qS{i}", [72, 2, 2, 512], BF16) for i in range(2)]
        NPT = 8
        PT = [sb(f"pt{i}", [128, 512], BF16) for i in range(NPT)]
        osb_tt = [es.enter_context(nc.sbuf_tensor(f"B{l}_osb{i}", [128, 8, 132], F32)) for i in range(2)]
        osb_b = [[Buf(osb_tt[i]), Buf(osb_tt[i])] for i in range(2)]
        rsm_b = [sb(f"rsm{i}", [128, 8], F32) for i in range(2)]
        t2 = sb("t2", [128, 128], F32)
        ab = sb("ab", [128, 128], F32)
        jk = sb("jk", [128, 128], F32)
        ssb = sb("ssb", [128, 8], F32)
        lnb = sb("lnb", [128, 8], F32)
        rstd = sb("rstd", [128, 8], F32)
        mst = [sb(f"mst{i}", [128, 4, 128], BF16) for i in range(2)]
        msl = [K.slot(f"Bm{i}") for i in range(2)]
        ksl = [K.slot(f"Bk{i}") for i in range(2)]
        qsl = [K.slot(f"Bq{i}") for i in range(2)]
        Sp = [psum(f"s{i}", [128, 512], F32) for i in range(4)]
        Op_t = es.enter_context(nc.psum_tensor(f"B{l}_o", [128, 8, 256], F32))
        Op = [Buf(Op_t), Buf(Op_t)]
        store_evs = []
        ones_ev = []
        for i in range(2):
            K.op('pool', lambda e: e.memset(vS[i].t[:, :, 128:132], 1.0), outs=[vS[i]])
            ones_ev.append(vS[i].ready[0])
        LA = 4
        nq = 0

        def load_kv(h):
            kb_, vb_, sl = kT[h % 2], vS[h % 2], ksl[h % 2]
            K.W('sp', kb_, vb_)
            for c in range(2):
                sl.dma('sp', kb_.t[0:64, c, :], T['KT'][h, c])
                sl.dma('sp', kb_.t[64:72, c, :], T['kaug'][h])
            kev = sl.dma('sp', vb_.t[:, :, 0:128], T['V'][:, h * 128:(h + 1) * 128].rearrange("(n p) d -> p n d", p=128))
            K.setw(kev, kb_)
            vb_.ready = [kev, ones_ev[h % 2]]
            vb_.readers = []

        def load_q(h, qt):
            b = qS[(h * 8 + qt) % 2]
            sl = qsl[(h * 8 + qt) % 2]
            K.W('sp', b)
            tok = slice(qt * 512, (qt + 1) * 512)
            for var in range(2):
                for c in range(2):
                    sl.dma('sp', b.t[0:64, var, c, :], T['QT'][h, c, :, tok])
                    ev = sl.dma('sp', b.t[64:72, var, c, :], T['qaug'][var, :, tok])
            K.setw(ev, b)

        tiles = []
        for h in range(4):
            for qt in range(8):
                kbs = kept_blocks(h, qt)
                for c in range(2):
                    for kb in kbs:
                        tiles.append((h, qt, c, kb, kb == kbs[0], kb == kbs[-1]))
        n = len(tiles)
        load_kv(0)
        load_q(0, 0)

        def qk(i):
            h, qt, c, kb, first, last = tiles[i]
            if c == 0 and first:
                if qt == 1 and h + 1 < 4:
                    load_kv(h + 1)
                nh, nqt = (h, qt + 1) if qt + 1 < 8 else (h + 1, 0)
                if nh < 4:
                    load_q(nh, nqt)
            kTb = kT[h % 2]
            qb_ = qS[(h * 8 + qt) % 2]
            sp = Sp[i % 4]
            K.R('pe', kTb, qb_)
            K.W('pe', sp)
            kcols = slice(kb * 128, (kb + 1) * 128)
            q0 = qt * 4
            if kb < q0 or kb >= q0 + 4:
                var = 0 if kb < q0 else 1
                ins = nc.tensor.matmul(sp.t[:], lhsT=kTb.t[0:72, c, kcols], rhs=qb_.t[0:72, var, c, :], start=True, stop=True)
            else:
                for qb in range(4):
                    cols = slice(qb * 128, (qb + 1) * 128)
                    gq = q0 + qb
                    if gq != kb:
                        var = 0 if gq > kb else 1
                        ins = nc.tensor.matmul(sp.t[:, cols], lhsT=kTb.t[0:72, c, kcols], rhs=qb_.t[0:72, var, c, cols],
                                               start=True, stop=True)
                    else:
                        nc.tensor.matmul(sp.t[:, cols], lhsT=kTb.t[0:64, c, kcols], rhs=qb_.t[0:64, 0, c, cols],
                                         start=True, stop=False)
                        ins = nc.tensor.matmul(sp.t[:, cols], lhsT=K.ident[:, :], rhs=bd.t[:, h, :], start=False, stop=True)
            ev = K.sig('pe', ins)
            K.setw(ev, sp)
            K.setr(ev, kTb, qb_)
            pt = PT[i % NPT]
            K.op('act', lambda e: e.activation(out=pt.t[:], in_=sp.t[:], func=AF.Exp, bias=sbias.t[:, 0:1]), outs=[pt], ins=[sp, sbias])

        def pv(i):
            h, qt, c, kb, first, last = tiles[i]
            pt = PT[i % NPT]
            vb_ = vS[h % 2]
            K.R('pe', pt, vb_)
            if first:
                K.W('pe', Op[c])
            for qb in range(4):
                slot = c * 4 + qb
                ins = nc.tensor.matmul(Op_t[:, slot, 0:129], lhsT=pt.t[:, qb * 128:(qb + 1) * 128], rhs=vb_.t[:, kb, 0:129],
                                       start=(first and qb % 2 == 0), stop=last, skip_group_check=True)
            ev = K.sig('pe', ins)
            K.setr(ev, pt, vb_)
            if last:
                K.setw(ev, Op[c])
                osb_t, osb = osb_tt[(h * 8 + qt) % 2], osb_b[(h * 8 + qt) % 2]
                K.op('dve', lambda e: e.tensor_copy(out=osb_t[:, c * 4:(c + 1) * 4, 0:129], in_=Op_t[:, c * 4:(c + 1) * 4, 0:129]),
                     outs=[osb[c]], ins=[Op[c]])
                if c == 1:
                    while deferred:
                        next(deferred.pop(0)[1])
                    st = finalize(qt, h)
                    next(st)
                    deferred.append((i + 12, st))

        def finalize(qt, h):
            nonlocal nq
            osb_t, osb, rsm = osb_tt[(h * 8 + qt) % 2], osb_b[(h * 8 + qt) % 2], rsm_b[(h * 8 + qt) % 2]
            K.op('dve', lambda e: e.reciprocal(out=rsm.t[:, 0:8], in_=osb_t[:, :, 128]), outs=[rsm], ins=[osb[0], osb[1]])
            K.op('dve', lambda e: e.tensor_scalar(out=rsm.t[:, 4:8], in0=rsm.t[:, 4:8], scalar1=lam.t[:, h:h + 1], scalar2=None,
                                                  op0=ALU.mult), outs=[rsm], ins=[rsm, lam])
            mb = mst[nq % 2]
            msl_ = msl[nq % 2]
            nq += 1
            for ps_ in range(2):
                for qb in range(4):
                    K.op('dve', lambda e: e.tensor_scalar(out=t2.t[:], in0=osb_t[:, 4 + qb, 0:128], scalar1=rsm.t[:, 4 + qb:5 + qb],
                                                          scalar2=None, op0=ALU.mult), outs=[t2], ins=[osb[1], rsm])
                    K.op('dve', lambda e: e.scalar_tensor_tensor(out=ab.t[:], in0=osb_t[:, qb, 0:128], scalar=rsm.t[:, qb:qb + 1],
                                                                 in1=t2.t[:], op0=ALU.mult, op1=ALU.subtract),
                         outs=[ab], ins=[osb[0], rsm, t2])
                    if ps_ == 0:
                        K.op('dve', lambda e: e.tensor_tensor(out=jk.t[:], in0=ab.t[:], in1=ab.t[:], op=ALU.mult), outs=[jk], ins=[ab])
                        K.op('dve', lambda e: e.tensor_reduce(out=ssb.t[:, qb:qb + 1], in_=jk.t[:], axis=AX.X, op=ALU.add),
                             outs=[ssb], ins=[jk], partial=(qb > 0))
                    else:
                        K.op('dve', lambda e: e.scalar_tensor_tensor(out=mb.t[:, qb, :], in0=ab.t[:], scalar=rstd.t[:, qb:qb + 1],
                                                                     in1=dog.t[:, h * 128:(h + 1) * 128], op0=ALU.mult, op1=ALU.mult),
                             outs=[mb], ins=[ab, rstd, dog], partial=(qb > 0))
                if ps_ == 0:
                    K.op('dve', lambda e: e.tensor_scalar(out=lnb.t[:, 0:4], in0=ssb.t[:, 0:4], scalar1=1.0 / 128, scalar2=EPS,
                                                          op0=ALU.mult, op1=ALU.add), outs=[lnb], ins=[ssb])
                    yield
                    K.op('act', lambda e: e.activation(out=lnb.t[:, 0:4], in_=lnb.t[:, 0:4], func=AF.Ln), outs=[lnb], ins=[lnb])
                    K.op('act', lambda e: e.activation(out=rstd.t[:, 0:4], in_=lnb.t[:, 0:4], func=AF.Exp, scale=-0.5),
                         outs=[rstd], ins=[lnb])
            K.R('pool', mb)
            dst = T['MIX'][qt * 512:(qt + 1) * 512, h * 128:(h + 1) * 128].rearrange("(q p) d -> p q d", p=128)
            ev = msl_.dma('pool', dst, mb.t[:])
            K.setr(ev, mb)
            store_evs.append(ev)
            yield

        deferred = []
        for i in range(n + LA):
            if i < n:
                qk(i)
            if i - LA >= 0:
                pv(i - LA)
                while deferred and deferred[0][0] <= i - LA:
                    next(deferred.pop(0)[1])
        while deferred:
            next(deferred.pop(0)[1])
        K.barrier(store_evs)


import os
CSTOP = int(os.environ.get('CSTOP', '99'))


def phase_C(K, l):
    nc, T = K.nc, K.T
    with ExitStack() as es:
        def sb(n, shp, dt):
            return Buf(es.enter_context(nc.sbuf_tensor(f"C{l}_{n}", shp, dt)))

        def psum(n, shp, dt):
            return Buf(es.enter_context(nc.psum_tensor(f"C{l}_{n}", shp, dt)))
        rc = sb("rc", [128, 5 * 128 + 2], F32)
        dec = sb("dec", [128, 16], F32)
        gng = sb("gng", [128, 512], F32)
        psl = K.slot(f"Cpar")
        psl.dma('sp', rc.t[:], T['rc'])
        psl.dma('sp', dec.t[:, 0:8], pvv(K, l, "rdf").partition_broadcast(128))
        psl.dma('sp', dec.t[:, 8:16], pvv(K, l, "rdb").partition_broadcast(128))
        pev = psl.dma('sp', gng.t[:], pvv(K, l, "gng").partition_broadcast(128))
        K.setw(pev, rc, dec, gng)
        lg = sb("lg", [128, 16], F32)
        lgsel = sb("lgsel", [128, 8], F32)
        gC = sb("gC", [128, 8], F32)
        arg = sb("arg", [128, 8, 2], F32)
        scK = sb("scK", [128, 8, 2], F32)
        K.op('act', lambda e: e.activation(out=lg.t[:], in_=dec.t[:], func=AF.Exp, scale=-1.0), outs=[lg], ins=[dec])
        K.op('dve', lambda e: e.tensor_scalar(out=lg.t[:], in0=lg.t[:], scalar1=1.0, scalar2=None, op0=ALU.add), outs=[lg], ins=[lg])
        K.op('act', lambda e: e.activation(out=lg.t[:], in_=lg.t[:], func=AF.Ln), outs=[lg], ins=[lg])
        K.op('dve', lambda e: e.tensor_scalar(out=lg.t[:], in0=lg.t[:], scalar1=-1.0, scalar2=None, op0=ALU.mult), outs=[lg], ins=[lg])
        K.op('dve', lambda e: e.tensor_copy(out=lgsel.t[0:64, :], in_=lg.t[0:64, 0:8]), outs=[lgsel], ins=[lg])
        K.op('dve', lambda e: e.tensor_copy(out=lgsel.t[64:128, :], in_=lg.t[64:128, 8:16]), outs=[lgsel], ins=[lg], partial=True)
        K.op('act', lambda e: e.activation(out=gC.t[:], in_=lgsel.t[:], func=AF.Exp, scale=128.0), outs=[gC], ins=[lgsel])
        cK = 5 * 128
        K.op('dve', lambda e: e.tensor_scalar(out=arg.t[:, :, 0], in0=lg.t[:, 0:8], scalar1=rc.t[:, cK:cK + 1], scalar2=None,
                                              op0=ALU.mult), outs=[arg], ins=[lg, rc])
        K.op('dve', lambda e: e.tensor_scalar(out=arg.t[:, :, 1], in0=lg.t[:, 8:16], scalar1=rc.t[:, cK + 1:cK + 2], scalar2=None,
                                              op0=ALU.mult), outs=[arg], ins=[lg, rc], partial=True)
        K.op('act', lambda e: e.activation(out=scK.t[:], in_=arg.t[:], func=AF.Exp), outs=[scK], ins=[arg])
        MT = sb("MT", [128, 128], F32)
        e1 = sb("e1", [128, 128], F32)
        e2 = sb("e2", [128, 128], F32)
        G = sb("G", [128, 128], BF16)
        gm = sb("gm", [128, 8, 32], F32)
        K.op('dve', lambda e: e.memset(gm.t[:], 0.0), outs=[gm])
        qT2 = sb("qT2", [128, S], BF16)
        kTr = sb("kTr", [64, S], BF16)
        qfb = sb("qfb", [128, S], BF16)
        ktm = sb("ktm", [128, 32, 64], BF16)
        kfb = sb("kfb", [128, 32, 128], BF16)
        vSb = [sb(f"vS{i}", [128, 32, 64], BF16) for i in range(2)]
        kvT = sb("kvT", [128, 64, 32], F32)
        Spb = [sb(f"Sperm{i}", [128, 32, 64], BF16) for i in range(2)]
        for Sp0 in Spb:
            K.op('pool', lambda e: e.memset(Sp0.t[0:64, 0, :], 0.0), outs=[Sp0])
            K.op('pool', lambda e: e.memset(Sp0.t[64:128, 31, :], 0.0), outs=[Sp0], partial=True)
        gate = [sb(f"gate{i}", [128, 8, 64], F32) for i in range(2)]
        A_sb = [sb(f"A{i}", [128, 128], BF16) for i in range(4)]
        ysb = sb("ysb", [128, 8, 64], F32)
        ysq = sb("ysq", [128, 8, 64], F32)
        s1 = sb("s1", [128, 8], F32)
        s2 = sb("s2", [128, 8], F32)
        var = sb("var", [128, 8], F32)
        nb = sb("nb", [128, 8], F32)
        gg = gate
        lnb = sb("lnb", [128, 8], F32)
        rstd = sb("rstd", [128, 8], F32)
        mst = [sb(f"mst{i}", [128, 8, 64], BF16) for i in range(2)]
        msl = [K.slot(f"Cm{i}") for i in range(2)]
        gsl = [K.slot(f"Cg{i}") for i in range(2)]
        hsl = K.slot(f"Ch")
        hsl2 = K.slot(f"Ch2")
        kvp = [psum(f"kvp{i}", [128, 8, 64], F32) for i in range(2)]
        scp = [psum(f"scp{i}", [128, 128], F32) for i in range(4)]
        yp = [psum(f"yp{i}", [128, 8, 64], F32) for i in range(2)]
        store_evs = []
        nmix = 0
        gn_def = []
        def tables(h):
            K.op('act', lambda e: e.activation(out=e1.t[:], in_=rc.t[:, 0:128], func=AF.Exp, scale=lg.t[:, h:h + 1]),
                 outs=[e1], ins=[rc, lg])
            K.op('act', lambda e: e.activation(out=e2.t[:], in_=rc.t[:, 256:384], func=AF.Exp, scale=lg.t[:, 8 + h:9 + h]),
                 outs=[e2], ins=[rc, lg])
            K.op('dve', lambda e: e.tensor_tensor(out=e1.t[:], in0=e1.t[:], in1=rc.t[:, 128:256], op=ALU.mult), outs=[e1], ins=[e1])
            K.op('dve', lambda e: e.tensor_tensor(out=e2.t[:], in0=e2.t[:], in1=rc.t[:, 384:512], op=ALU.mult), outs=[e2], ins=[e2])
            K.op('dve', lambda e: e.tensor_tensor(out=MT.t[:], in0=e1.t[:], in1=e2.t[:], op=ALU.add), outs=[MT], ins=[e1, e2])
            K.op('act', lambda e: e.activation(out=G.t[:], in_=rc.t[:, 512:640], func=AF.Exp, scale=lgsel.t[:, h:h + 1]),
                 outs=[G], ins=[rc, lgsel])

        def pro_steps(h):
            vS_, Spn = vSb[h % 2], Spb[h % 2]

            def s_load():
                K.W('sp', ktm, vS_)
                hsl.dma('sp', ktm.t[:], T['RK'][:, h * 64:(h + 1) * 64].rearrange("(n p) d -> p n d", p=128))
                hev = hsl.dma('sp', vS_.t[:], T['RV'][:, h * 64:(h + 1) * 64].rearrange("(n p) d -> p n d", p=128))
                K.setw(hev, ktm, vS_)

            def s_kfb():
                for d in range(2):
                    K.op('dve', lambda e: e.tensor_scalar(out=kfb.t[:, :, d * 64:(d + 1) * 64], in0=ktm.t[:], scalar1=scK.t[:, h, d:d + 1],
                                                          scalar2=None, op0=ALU.mult), outs=[kfb], ins=[ktm, scK], partial=(d > 0))

            def s_p1(g):
                def run():
                    kp = kvp[g % 2]
                    K.R('pe', kfb, vS_)
                    K.W('pe', kp)
                    for j in range(8):
                        nn = g * 8 + j
                        ins = nc.tensor.matmul(kp.t[:, j, :], lhsT=kfb.t[:, nn, :], rhs=vS_.t[:, nn, :], start=True, stop=True)
                    ev = K.sig('pe', ins)
                    K.setw(ev, kp)
                    K.setr(ev, kfb, vS_)
                    K.op('dve', lambda e: e.tensor_copy(out=kvT.t[0:64, :, g * 8:(g + 1) * 8].rearrange("p e t -> p t e"),
                                                        in_=kp.t[0:64, :, :]), outs=[kvT], ins=[kp], partial=(g > 0))
                    b0 = kvT.t[64:128, 0, 31 - g * 8:32 - g * 8]
                    rev_out = bass.AP(kvT.t, b0.offset, [list(b0.ap[0]), [-1, 8], [32, 64]])
                    K.op('act', lambda e: e.copy(out=rev_out, in_=kp.t[64:128, :, :]), outs=[kvT], ins=[kp], partial=True)
                return run

            def s_scan():
                K.op('dve', lambda e: e.tensor_scalar(out=gm.t[:, :, 1:32], in0=gm.t[:, :, 1:32], scalar1=0.0, scalar2=gC.t[:, h:h + 1],
                                                      op0=ALU.mult, op1=ALU.add), outs=[gm], ins=[gm, gC])
                K.R('dve', gm)
                K.W('dve', kvT)
                for e8 in range(8):
                    ins = nc.vector.tensor_tensor_scan(out=kvT.t[:, e8 * 8:(e8 + 1) * 8, :].rearrange("p e t -> p (e t)"),
                                                       data0=gm.t[:].rearrange("p e t -> p (e t)"),
                                                       data1=kvT.t[:, e8 * 8:(e8 + 1) * 8, :].rearrange("p e t -> p (e t)"), initial=0.0,
                                                       op0=ALU.mult, op1=ALU.add)
                ev = K.sig('dve', ins)
                K.setw(ev, kvT)
                K.setr(ev, gm)

            def s_sperm():
                K.op('dve', lambda e: e.tensor_copy(out=Spn.t[0:64, 1:32, :], in_=kvT.t[0:64, :, 0:31].rearrange("p e t -> p t e")),
                     outs=[Spn], ins=[kvT])
                b1 = kvT.t[64:128, 0, 30:31]
                rev_in = bass.AP(kvT.t, b1.offset, [list(b1.ap[0]), [-1, 31], [32, 64]])
                K.op('act', lambda e: e.copy(out=Spn.t[64:128, 0:31, :], in_=rev_in), outs=[Spn], ins=[kvT], partial=True)
            return [s_load, s_kfb, s_p1(0), s_p1(1), s_p1(2), s_p1(3), s_scan, s_sperm]

        SCHED = {0: [0], 2: [1], 4: [2], 7: [3], 10: [4], 13: [5], 17: [6], 21: [7]}
        for st_ in pro_steps(0):
            st_()
        for h in range(8):
            vS, Sp_ = vSb[h % 2], Spb[h % 2]
            nxt = pro_steps(h + 1) if h + 1 < 8 else None
            tables(h)
            K.W('sp', qT2, kTr)
            hsl2.dma('sp', qT2.t[0:64, :], T['RQT'][h])
            hsl2.dma('sp', qT2.t[64:128, :], T['RQT'][h])
            hev = hsl2.dma('sp', kTr.t[:, :], T['RKT'][h])
            K.setw(hev, qT2, kTr)
            K.op('dve', lambda e: e.tensor_tensor(out=qfb.t[:].rearrange("p (n c) -> p n c", n=32),
                                                   in0=qT2.t[:].rearrange("p (n c) -> p n c", n=32),
                                                   in1=G.t[:].unsqueeze(1).to_broadcast([128, 32, 128]), op=ALU.mult),
                 outs=[qfb], ins=[qT2, G])
            DEPTH = 3

            def scores(nn):
                cols = slice(nn * 128, (nn + 1) * 128)
                sp = scp[nn % 4]
                K.R('pe', kTr, qT2)
                K.W('pe', sp)
                ev = K.sig('pe', nc.tensor.matmul(sp.t[:], lhsT=kTr.t[0:64, cols], rhs=qT2.t[0:64, cols], start=True, stop=True))
                K.setw(ev, sp)
                K.setr(ev, kTr, qT2)
                ab = A_sb[nn % 4]
                K.op('dve', lambda e: e.tensor_tensor(out=ab.t[:], in0=sp.t[:], in1=MT.t[:], op=ALU.mult), outs=[ab], ins=[sp, MT])

            def ymm(pn, h=h):
                nonlocal nmix
                g, pj = divmod(pn, 8)
                ypb = yp[g % 2]
                pab = A_sb[pn % 4]
                pc = slice(pn * 128, (pn + 1) * 128)
                K.R('pe', pab, vS, qfb, Sp_)
                if pj == 0:
                    K.W('pe', ypb)
                mms = [(pab.t[:], vS.t[:, pn, :]), (qfb.t[:, pc], Sp_.t[:, pn, :])]
                for mi, (lt, rh) in enumerate(mms):
                    ins = nc.tensor.matmul(ypb.t[:, pj, :], lhsT=lt, rhs=rh, start=(mi == 0), stop=(mi == len(mms) - 1))
                ev = K.sig('pe', ins)
                K.setr(ev, pab, vS, qfb, Sp_)
                if pj != 7:
                    return
                K.setw(ev, ypb)
                st_ = gn_tail(g, ypb, h)
                next(st_)
                gn_def.append((pn + 2, st_, h))

            def gn_tail(g, ypb, h):
                nonlocal nmix
                gb, ggb = gate[(h * 4 + g) % 2], gg[(h * 4 + g) % 2]
                K.op('act', lambda e: e.copy(out=ysb.t[:], in_=ypb.t[:]), outs=[ysb], ins=[ypb])
                K.op('act', lambda e: e.activation(out=ysq.t[:], in_=ysb.t[:], func=AF.Square), outs=[ysq], ins=[ysb])
                K.op('dve', lambda e: e.tensor_reduce(out=s1.t[:], in_=ysb.t[:], axis=AX.X, op=ALU.add), outs=[s1], ins=[ysb])
                K.op('dve', lambda e: e.tensor_reduce(out=s2.t[:], in_=ysq.t[:], axis=AX.X, op=ALU.add), outs=[s2], ins=[ysq])
                K.op('dve', lambda e: e.tensor_scalar(out=s1.t[:], in0=s1.t[:], scalar1=1.0 / 64, scalar2=None, op0=ALU.mult), outs=[s1], ins=[s1])
                K.op('dve', lambda e: e.tensor_tensor(out=var.t[:], in0=s1.t[:], in1=s1.t[:], op=ALU.mult), outs=[var], ins=[s1])
                K.op('dve', lambda e: e.scalar_tensor_tensor(out=var.t[:], in0=s2.t[:], scalar=1.0 / 64, in1=var.t[:], op0=ALU.mult,
                                                             op1=ALU.subtract), outs=[var], ins=[s2, var])
                K.op('dve', lambda e: e.tensor_scalar(out=lnb.t[:, 0:8], in0=var.t[:, 0:8], scalar1=1.0, scalar2=EPS,
                                                      op0=ALU.mult, op1=ALU.add), outs=[lnb], ins=[var])
                yield 2
                K.op('act', lambda e: e.activation(out=lnb.t[:, 0:8], in_=lnb.t[:, 0:8], func=AF.Ln), outs=[lnb], ins=[lnb])
                K.op('act', lambda e: e.activation(out=rstd.t[:, 0:8], in_=lnb.t[:, 0:8], func=AF.Exp, scale=-0.5), outs=[rstd], ins=[lnb])
                yield 2
                K.op('dve', lambda e: e.scalar_tensor_tensor(out=nb.t[:], in0=s1.t[:], scalar=-1.0, in1=rstd.t[:], op0=ALU.mult,
                                                             op1=ALU.mult), outs=[nb], ins=[s1, rstd])
                for jj in range(8):
                    if jj % 2 == 0:
                        K.op('act', lambda e: e.activation(out=ysq.t[:, jj, :], in_=ysb.t[:, jj, :], func=AF.Identity,
                                                           scale=rstd.t[:, jj:jj + 1], bias=nb.t[:, jj:jj + 1]),
                             outs=[ysq], ins=[ysb, nb, rstd], partial=(jj > 0))
                    else:
                        K.op('dve', lambda e: e.tensor_scalar(out=ysq.t[:, jj, :], in0=ysb.t[:, jj, :], scalar1=s1.t[:, jj:jj + 1],
                                                              scalar2=rstd.t[:, jj:jj + 1], op0=ALU.subtract, op1=ALU.mult),
                             outs=[ysq], ins=[ysb, s1, rstd], partial=True)
                yield 3
                mb = mst[nmix % 2]
                ms_ = msl[nmix % 2]
                nmix += 1
                K.op('dve', lambda e: e.tensor_tensor(out=mb.t[:], in0=ysq.t[:], in1=ggb.t[:], op=ALU.mult), outs=[mb], ins=[ysq, ggb])
                K.R('pool', mb)
                dst = T['MIX'][g * 1024:(g + 1) * 1024, 512 + h * 64:512 + (h + 1) * 64].rearrange("(n p) d -> p n d", p=128)
                ev = ms_.dma('pool', dst, mb.t[:])
                K.setr(ev, mb)
                store_evs.append(ev)
                yield

            for nn in range(32 + DEPTH):
                if nxt is not None and nn in SCHED:
                    for si in SCHED[nn]:
                        nxt[si]()
                if nn < 32:
                    if nn % 8 == 2:
                        g = nn // 8
                        gb, ggb = gate[(h * 4 + g) % 2], gg[(h * 4 + g) % 2]
                        gs = gsl[(h * 4 + g) % 2]
                        K.W('sp', gb)
                        gev = gs.dma('sp', gb.t[:], T['GATE'][g * 1024:(g + 1) * 1024, h * 64:(h + 1) * 64].rearrange("(n p) d -> p n d", p=128))
                        K.setw(gev, gb)
                        K.op('pool', lambda e: e.tensor_tensor(out=gb.t[:], in0=gb.t[:],
                                                               in1=gng.t[:, h * 64:(h + 1) * 64].unsqueeze(1).to_broadcast([128, 8, 64]),
                                                               op=ALU.mult), outs=[gb], ins=[gb, gng])
                    scores(nn)
                if nn - DEPTH >= 0:
                    ymm(nn - DEPTH)
                    now = nn - DEPTH
                    for _ in range(len(gn_def)):
                        due, gen_, hh = gn_def[0]
                        if (hh == h and due <= now) or (hh != h and nn >= 2):
                            gn_def.pop(0)
                            try:
                                dly = next(gen_)
                                gn_def.append((now + (dly or 1), gen_, h))
                            except StopIteration:
                                pass
                        else:
                            break
        while gn_def:
            gen_ = gn_def.pop(0)[1]
            for _ in gen_:
                pass
        K.barrier(store_evs)


def phase_D(K, l, cast_ev):
    nc, T = K.nc, K.T
    with ExitStack() as es:
        def sb(n, shp, dt):
            return Buf(es.enter_context(nc.sbuf_tensor(f"D{l}_{n}", shp, dt)))

        def psum(n, shp, dt):
            return Buf(es.enter_context(nc.psum_tensor(f"D{l}_{n}", shp, dt)))
        mixb = [sb(f"mix{i}", [128, 4, 1024], BF16) for i in range(2)]
        mT = [sb(f"mT{i}", [128, 8, 512], BF16) for i in range(2)]
        Wb = [sb(f"W{i}", [128, 8, 512], BF16) for i in range(2)]
        Wsl = [K.slot(f"DW{i}") for i in range(2)]
        msl = [K.slot(f"Dm{i}") for i in range(2)]
        tp = [psum(f"tp{i}", [128, 8, 128], BF16) for i in range(2)]
        mm = [psum(f"mm{i}", [128, 512], F32) for i in range(4)]
        wv = T['wb_out'][l].rearrange("(k p) c -> p k c", p=128)
        chunks = [(tg, ct) for tg in range(8) for ct in range(2)]
        nmm = 0

        def issue(i):
            tg, ct = chunks[i]
            b = Wb[i % 2]
            K.W('sp', b)
            if i == 0:
                K.wait('sp', cast_ev)
            ev = Wsl[i % 2].dma('sp', b.t[:], wv[:, :, ct * 512:(ct + 1) * 512])
            K.setw(ev, b)

        def load_mix(tg):
            b = mixb[tg % 2]
            K.W('sp', b)
            ev = msl[tg % 2].dma('sp', b.t[:], T['MIX'][tg * 512:(tg + 1) * 512, :].rearrange("(q p) d -> p q d", p=128))
            K.setw(ev, b)

        def transpose_group(tg):
            b = mixb[tg % 2]
            mt = mT[tg % 2]
            for tt in range(4):
                tpb = tp[tt % 2]
                K.R('pe', b)
                K.W('pe', tpb)
                for k in range(8):
                    ins = nc.tensor.transpose(out=tpb.t[:, k, :], in_=b.t[:, tt, k * 128:(k + 1) * 128], identity=K.ident[:])
                ev = K.sig('pe', ins)
                K.setw(ev, tpb)
                K.setr(ev, b)
                K.op('act', lambda e: e.copy(out=mt.t[:, :, tt * 128:(tt + 1) * 128], in_=tpb.t[:]), outs=[mt], ins=[tpb],
                     partial=(tt > 0))
        load_mix(0)
        load_mix(1)
        issue(0)
        transpose_group(0)
        for i, (tg, ct) in enumerate(chunks):
            if i + 1 < len(chunks):
                issue(i + 1)
            if ct == 1 and tg + 1 < 8:
                transpose_group(tg + 1)
                if tg + 2 < 8:
                    load_mix(tg + 2)
            wb = Wb[i % 2]
            mt = mT[tg % 2]
            for tt in range(4):
                t = tg * 4 + tt
                m = mm[nmm % 4]
                nmm += 1
                K.R('pe', wb, mt)
                K.W('pe', m)
                for k in range(8):
                    ins = nc.tensor.matmul(m.t[:], lhsT=mt.t[:, k, tt * 128:(tt + 1) * 128], rhs=wb.t[:, k, :], start=(k == 0), stop=(k == 7))
                ev = K.sig('pe', ins)
                K.setw(ev, m)
                K.setr(ev, wb, mt)
                xt = K.x[:, t, ct * 512:(ct + 1) * 512]
                K.R('dve', m)
                ev = K.sig('dve', nc.vector.tensor_tensor(out=xt, in0=m.t[:], in1=xt, op=ALU.add))
                K.setr(ev, m)
        K.barrier()


def phase_E(K, l, cast1, cast2):
    nc, T = K.nc, K.T
    with ExitStack() as es:
        def sb(n, shp, dt):
            return Buf(es.enter_context(nc.sbuf_tensor(f"E{l}_{n}", shp, dt)))

        def psum(n, shp, dt):
            return Buf(es.enter_context(nc.psum_tensor(f"E{l}_{n}", shp, dt)))
        g2b = sb("g2b", [128, 1024], F32)
        psl = K.slot(f"Epar")
        pev = psl.dma('sp', g2b.t[:], pvv(K, l, "n2").partition_broadcast(128))
        K.setw(pev, g2b)
        hbf = [sb(f"hbf{i}", [128, 1024], BF16) for i in range(2)]
        hT = [sb(f"hT{i}", [128, 8, 512], BF16) for i in range(2)]
        uT = sb("uT", [128, 32, 512], BF16)
        Wb = [sb(f"W{i}", [128, 8, 512], BF16) for i in range(2)]
        Wsl = [K.slot(f"EW{i}") for i in range(2)]
        r32 = [sb(f"r32{i}", [128, 512], F32) for i in range(2)]
        st = [sb(f"st{i}", [128, 12], F32) for i in range(2)]
        mv = [sb(f"mv{i}", [128, 2], F32) for i in range(2)]
        ms = [sb(f"ms{i}", [128, 1], F32) for i in range(2)]
        lnb = [sb(f"lnb{i}", [128, 8], F32) for i in range(2)]
        rs = [sb(f"rs{i}", [128, 8], F32) for i in range(2)]
        tp = [psum(f"tp{i}", [128, 8, 128], BF16) for i in range(2)]
        hp = [psum(f"hp{i}", [128, 512], F32) for i in range(2)]
        acc = [psum(f"acc{i}", [128, 512], F32) for i in range(4)]
        w1v = T['wb1'][l].rearrange("(k p) c -> p k c", p=128)
        w2v = T['wb2'][l].rearrange("(f p) c -> p f c", p=128)
        chunks = []
        for tg in range(8):
            chunks += [(tg, 1, fc, 0) for fc in range(8)]
            chunks += [(tg, 2, fg, ct) for ct in range(2) for fg in range(8)]

        def issue(i):
            tg, kind, a, ct = chunks[i]
            b = Wb[i % 2]
            K.W('sp', b)
            if i == 0:
                K.wait('sp', cast1)
                K.wait('sp', cast2)
            if kind == 1:
                ev = Wsl[i % 2].dma('sp', b.t[:], w1v[:, :, a * 512:(a + 1) * 512])
            else:
                ev = Wsl[i % 2].dma('sp', b.t[:, 0:4, :], w2v[:, a * 4:(a + 1) * 4, ct * 512:(ct + 1) * 512])
            K.setw(ev, b)

        chain_a, chain_b, tr = make_norm(K, g2b, hbf, hT, tp, st, mv, ms, lnb, rs)
        issue(0)
        for t0_ in (0, 2):
            chain_a(0, t0_)
            chain_a(0, t0_ + 1)
            chain_b(0, t0_)
            chain_b(0, t0_ + 1)
            tr(0, t0_)
            tr(0, t0_ + 1)
        nh = 0
        for i, (tg, kind, a, ct) in enumerate(chunks):
            if i + 1 < len(chunks):
                issue(i + 1)
            gi = i % 24
            if tg + 1 < 8 and gi >= 8 and (gi - 8) % 3 == 0 and (gi - 8) // 3 < 4:
                chain_a(tg + 1, (gi - 8) // 3)
            if tg + 1 < 8 and gi >= 9 and (gi - 9) % 3 == 0 and (gi - 9) // 3 < 4:
                chain_b(tg + 1, (gi - 9) // 3)
            if tg + 1 < 8 and gi >= 11 and (gi - 11) % 3 == 0 and (gi - 11) // 3 < 4:
                tr(tg + 1, (gi - 11) // 3)
            wb = Wb[i % 2]
            hb = hT[tg % 2]
            if kind == 1:
                for f4 in range(4):
                    f = a * 4 + f4
                    p = hp[nh % 2]
                    r = r32[nh % 2]
                    nh += 1
                    K.R('pe', wb, hb)
                    K.W('pe', p)
                    for k in range(8):
                        ins = nc.tensor.matmul(p.t[:], lhsT=wb.t[:, k, f4 * 128:(f4 + 1) * 128], rhs=hb.t[:, k, :], start=(k == 0), stop=(k == 7))
                    ev = K.sig('pe', ins)
                    K.setw(ev, p)
                    K.setr(ev, wb, hb)
                    K.op('act', lambda e: e.activation(out=r.t[:], in_=p.t[:], func=AF.Relu), outs=[r], ins=[p])
                    K.op('dve', lambda e: e.tensor_tensor(out=uT.t[:, f, :], in0=r.t[:], in1=r.t[:], op=ALU.mult), outs=[uT], ins=[r],
                         partial=(f > 0))
            else:
                K.R('pe', wb, uT)
                for fi in range(4):
                    f = a * 4 + fi
                    for tt in range(4):
                        if f == 0:
                            K.W('pe', acc[tt])
                        ins = nc.tensor.matmul(acc[tt].t[:], lhsT=uT.t[:, f, tt * 128:(tt + 1) * 128], rhs=wb.t[:, fi, :],
                                               start=(f == 0), stop=(f == 31))
                        if f == 31:
                            ev = K.sig('pe', ins)
                            K.setw(ev, acc[tt])
                if a != 7:
                    ev = K.sig('pe', ins)
                K.setr(ev, wb, uT)
                if a == 7:
                    for tt in range(4):
                        t = tg * 4 + tt
                        xt = K.x[:, t, ct * 512:(ct + 1) * 512]
                        K.R('dve', acc[tt])
                        ev2 = K.sig('dve', nc.vector.tensor_tensor(out=xt, in0=acc[tt].t[:], in1=xt, op=ALU.add))
                        K.setr(ev2, acc[tt])
        K.barrier()


def _consts():
    bf = ml_dtypes.bfloat16
    pos = np.arange(S)
    ih = (pos // 128).astype(np.float32)
    il = (pos % 128).astype(np.float32)
    one = np.ones(S, np.float32)
    zero = np.zeros(S, np.float32)
    qaug = np.stack([np.stack([ih, il, one, one, zero, zero, zero, zero]),
                     np.stack([zero, zero, zero, zero, ih, il, one, one])]).astype(bf)
    kaug = []
    for s in SLOPES:
        a = np.stack([-128 * s * one, -s * one, 128 * s * ih, s * il])
        kaug.append(np.concatenate([a, -a], axis=0))
    kaug = np.stack(kaug).astype(bf)
    j = np.arange(128)[:, None].astype(np.float32)
    c = np.arange(128)[None, :].astype(np.float32)
    bd = np.stack([-s * np.abs(c - j) for s in SLOPES], axis=1).astype(bf)
    rc = np.zeros((128, 5 * 128 + 2), np.float32)
    rc[:, 0:128] = np.maximum(c - j, 0)
    rc[:, 128:256] = (c >= j)
    rc[:, 256:384] = np.maximum(j - c, 0)
    rc[:, 384:512] = (j > c)
    rc[0:64, 512:640] = c + 1.0
    rc[64:128, 512:640] = 128.0 - c
    rc[:, 640] = 127.0 - j[:, 0]
    rc[:, 641] = j[:, 0]
    return dict(qaug=qaug, kaug=kaug, bd=bd, rc=rc)


_PROGS = {}


def _get_prog(L, lam_layers, debug=False):
    key = (L, tuple(lam_layers), debug)
    if key not in _PROGS:
        _PROGS[key] = build_program(L, lam_layers, debug)
    return _PROGS[key]


def _pack_pv(inp, layers):
    rows = []
    for l in layers:
        rows.append(np.concatenate([
            inp['norm1_g'][l], inp['norm2_g'][l], inp['q_norm_g'][l], inp['k_norm_g'][l],
            inp['lambda_q1'][l].reshape(-1), inp['lambda_k1'][l].reshape(-1), inp['lambda_q2'][l].reshape(-1),
            inp['lambda_k2'][l].reshape(-1), inp['diff_out_g'][l], inp['ret_decay_fwd'][l], inp['ret_decay_bwd'][l],
            inp['ret_gn_g'][l]]).astype(np.float32))
    return np.ascontiguousarray(np.stack(rows))


FUSED = True


def kernel(**inp):
    inp = {k: np.asarray(v) for k, v in inp.items()}
    x = np.ascontiguousarray(inp['x'], dtype=np.float32)
    B = x.shape[0]
    cst = _consts()
    groups = [list(range(4))] if FUSED else [[l] for l in range(4)]
    cur = [x[b] for b in range(B)]
    for layers in groups:
        nc = _get_prog(len(layers), layers)
        shared = dict(w_in=np.ascontiguousarray(inp['w_in'][layers]), w_out=np.ascontiguousarray(inp['w_out'][layers]),
                      w1=np.ascontiguousarray(inp['w_mlp1'][layers]), w2=np.ascontiguousarray(inp['w_mlp2'][layers]),
                      pv=_pack_pv(inp, layers), **cst)
        in_maps = [dict(x=np.ascontiguousarray(cur[b]), **shared) for b in range(B)]
        res = run_bass_kernel_spmd(nc, in_maps, core_ids=list(range(B)))
        cur = [np.asarray(res.results[b]['y'], dtype=np.float32) for b in range(B)]
    return np.stack(cur).astype(np.float32)
```
